# Optimizing a Trainium2 kernel written in Bass

```python
import jax, jax.numpy as jnp
from jax import lax
import numpy as np

D_MODEL = 2048
BATCH = 1
SEQ = 8192
DEPTH = 1

CTX_LEN = 256
GRID_W = 64
D_MIX = D_MODEL
GLA_HEADS = 4
GLA_DK = 128
GLA_DV = 256
GLA_KW = GLA_HEADS * GLA_DK
GLA_VW = GLA_HEADS * GLA_DV
GATE_RANK = 16
GATE_NORM = 16.0
CHUNK = 64
CONV_W = D_MIX - GLA_VW
CONV_K = 3
FFN_HIDDEN = -(-8 * D_MODEL // (3 * 256)) * 256
IN_COLS = 2 * GLA_KW + 2 * GLA_VW + 2 * GATE_RANK + 3 * CONV_W
EPS = 1e-6

kernel_name = "hymba_style_gla_shortconv_flow_block"


def rmsnorm(x, g):
    x32 = x.astype(jnp.float32)
    y = x32 * lax.rsqrt(jnp.mean(x32 * x32, axis=-1, keepdims=True) + EPS)
    return y.astype(x.dtype) * g


def modulate(h, shift, scale):
    return h * (1.0 + scale) + shift


def split_proj(p):
    sizes = [GLA_KW, GLA_KW, GLA_VW, GLA_VW, GATE_RANK, GATE_RANK, CONV_W, CONV_W, CONV_W]
    idx = np.cumsum(sizes)[:-1].tolist()
    return jnp.split(p, idx, axis=-1)


def gla_chunked(q, k, v, log_a, s0):
    b_, n, h, _ = q.shape
    dv = v.shape[-1]
    nc = n // CHUNK

    def blk(t):
        return t.reshape(b_, nc, CHUNK, h, t.shape[-1]).transpose(0, 3, 1, 2, 4)

    q, k, v, log_a = blk(q), blk(k), blk(v), blk(log_a)
    cum = jnp.cumsum(log_a, axis=3)
    last = cum[:, :, :, -1:, :]
    q_e = q * jnp.exp(cum)
    k_e = k * jnp.exp(-cum)
    tri = jnp.tril(jnp.ones((CHUNK, CHUNK), dtype=bool))
    scores = jnp.where(tri, jnp.einsum('bhnid,bhnjd->bhnij', q_e, k_e), 0.0)
    o_intra = jnp.einsum('bhnij,bhnjv->bhniv', scores, v)
    ds = jnp.einsum('bhnjd,bhnjv->bhndv', k * jnp.exp(last - cum), v)
    decay = jnp.exp(last[:, :, :, 0, :])

    def step(s, inp):
        d, u = inp
        return d[..., None] * s + u, s

    s_final, s_prev = lax.scan(step, s0, (jnp.moveaxis(decay, 2, 0), jnp.moveaxis(ds, 2, 0)))
    s_prev = jnp.moveaxis(s_prev, 0, 2)
    o = o_intra + jnp.einsum('bhnid,bhndv->bhniv', q_e, s_prev)
    return o.transpose(0, 2, 3, 1, 4).reshape(b_, n, h, dv), s_final


def gla_bidir(q, k, v, la_f, la_b, s0_f, s0_b):
    o_f, s_f = gla_chunked(q, k, v, la_f, s0_f)
    flip = lambda t: jnp.flip(t, axis=1)
    o_b, s_b = gla_chunked(flip(q), flip(k), flip(v), flip(la_b), s0_b)
    return o_f + flip(o_b), s_f, s_b


def gla_inputs(q, k, v, lf, lb, w_gf, b_gf, w_gb, b_gb):
    b_, n = q.shape[:2]
    heads = lambda t, d: t.reshape(b_, n, GLA_HEADS, d).astype(jnp.float32)
    qh = heads(q, GLA_DK) * (GLA_DK ** -0.5)
    kh = heads(k, GLA_DK)
    vh = heads(v, GLA_DV)
    la_f = heads(jax.nn.log_sigmoid((lf @ w_gf + b_gf).astype(jnp.float32)) / GATE_NORM, GLA_DK)
    la_b = heads(jax.nn.log_sigmoid((lb @ w_gb + b_gb).astype(jnp.float32)) / GATE_NORM, GLA_DK)
    return qh, kh, vh, la_f, la_b


def conv_grid(u, w):
    b_, n, ch = u.shape
    rows = n // GRID_W
    gp = jnp.pad(u.reshape(b_, rows, GRID_W, ch), ((0, 0), (0, 0), (CONV_K // 2, CONV_K // 2), (0, 0)))
    y = sum(w[j] * gp[:, :, j:j + GRID_W] for j in range(CONV_K))
    return y.reshape(b_, n, ch)


def conv_seq(u, w):
    n = u.shape[1]
    up = jnp.pad(u, ((0, 0), (CONV_K // 2, CONV_K // 2), (0, 0)))
    return sum(w[j] * up[:, j:j + n] for j in range(CONV_K))


def mix_output(o_gla, g, cb, cc, cx, conv_fn, conv_w, gla_g, w_out):
    b_, n = g.shape[:2]
    o32 = o_gla * lax.rsqrt(jnp.mean(o_gla * o_gla, axis=-1, keepdims=True) + EPS)
    o = (o32.astype(g.dtype) * gla_g).reshape(b_, n, GLA_VW) * jax.nn.silu(g)
    y_conv = cb * conv_fn(cc * cx, conv_w)
    return jnp.concatenate([o, y_conv], axis=-1) @ w_out


def swiglu(h, wg, wu, wd):
    return (jax.nn.silu(h @ wg) * (h @ wu)) @ wd


def setup_inputs(seed: int = 0) -> dict:
    key = jax.random.key(seed)
    ks = jax.random.split(key, 24)
    nrm = lambda k, shape, s: jax.random.normal(k, shape, jnp.float32) * s
    D, L = D_MODEL, DEPTH
    return {
        "x": nrm(ks[0], (BATCH, SEQ, D), 1.0),
        "c": nrm(ks[1], (BATCH, D), 1.0),
        "ctx": nrm(ks[2], (BATCH, CTX_LEN, D), 1.0),
        "c_ctx": nrm(ks[3], (D,), 1.0),
        "w_mod": nrm(ks[4], (L, D, 6 * D), 0.5 * D ** -0.5),
        "b_mod": nrm(ks[5], (L, 6 * D), 0.01),
        "norm1_g": 1.0 + nrm(ks[6], (L, D), 0.05),
        "norm2_g": 1.0 + nrm(ks[7], (L, D), 0.05),
        "w_in": nrm(ks[8], (L, D, IN_COLS), D ** -0.5),
        "w_gate_f": nrm(ks[9], (L, GATE_RANK, GLA_KW), GATE_RANK ** -0.5),
        "b_gate_f": nrm(ks[10], (L, GLA_KW), 0.1),
        "w_gate_b": nrm(ks[11], (L, GATE_RANK, GLA_KW), GATE_RANK ** -0.5),
        "b_gate_b": nrm(ks[12], (L, GLA_KW), 0.1),
        "gla_norm_g": 1.0 + nrm(ks[13], (L, GLA_DV), 0.05),
        "conv_w": nrm(ks[14], (L, CONV_K, CONV_W), CONV_K ** -0.5),
        "w_out": nrm(ks[15], (L, D_MIX, D), D_MIX ** -0.5),
        "w_ffn_gate": nrm(ks[16], (L, D, FFN_HIDDEN), D ** -0.5),
        "w_ffn_up": nrm(ks[17], (L, D, FFN_HIDDEN), D ** -0.5),
        "w_ffn_down": nrm(ks[18], (L, FFN_HIDDEN, D), FFN_HIDDEN ** -0.5),
        "final_g": 1.0 + nrm(ks[19], (D,), 0.05),
    }


def reference(x, c, ctx, c_ctx, w_mod, b_mod, norm1_g, norm2_g, w_in, w_gate_f, b_gate_f,
              w_gate_b, b_gate_b, gla_norm_g, conv_w, w_out, w_ffn_gate, w_ffn_up, w_ffn_down,
              final_g):
    b_ = x.shape[0]
    for i in range(DEPTH):
        mx = jax.nn.silu(c) @ w_mod[i] + b_mod[i]
        mc = jax.nn.silu(c_ctx) @ w_mod[i] + b_mod[i]
        sh1, sc1, gt1, sh2, sc2, gt2 = [t[:, None, :] for t in jnp.split(mx, 6, axis=-1)]
        sh1c, sc1c, gt1c, sh2c, sc2c, gt2c = jnp.split(mc, 6, axis=-1)

        hc = modulate(rmsnorm(ctx, norm1_g[i]), sh1c, sc1c)
        qc, kc, vc, gc, lfc, lbc, cbc, ccc, cxc = split_proj(hc @ w_in[i])
        qh, kh, vh, la_f, la_b = gla_inputs(qc, kc, vc, lfc, lbc, w_gate_f[i], b_gate_f[i],
                                             w_gate_b[i], b_gate_b[i])
        s0 = jnp.zeros((b_, GLA_HEADS, GLA_DK, GLA_DV), jnp.float32)
        o_c, s_f, s_b = gla_bidir(qh, kh, vh, la_f, la_b, s0, s0)

        hx = modulate(rmsnorm(x, norm1_g[i]), sh1, sc1)
        q, k, v, g, lf, lb, cb, cc, cx = split_proj(hx @ w_in[i])
        qh, kh, vh, la_f, la_b = gla_inputs(q, k, v, lf, lb, w_gate_f[i], b_gate_f[i],
                                             w_gate_b[i], b_gate_b[i])
        o_x, _, _ = gla_bidir(qh, kh, vh, la_f, la_b, s_f, s_b)
        mix = mix_output(o_x, g, cb, cc, cx, conv_grid, conv_w[i], gla_norm_g[i], w_out[i])
        x_next = x + gt1 * mix
        h2 = modulate(rmsnorm(x_next, norm2_g[i]), sh2, sc2)
        x_next = x_next + gt2 * swiglu(h2, w_ffn_gate[i], w_ffn_up[i], w_ffn_down[i])

        if i < DEPTH - 1:
            mix_c = mix_output(o_c, gc, cbc, ccc, cxc, conv_seq, conv_w[i], gla_norm_g[i], w_out[i])
            ctx = ctx + gt1c * mix_c
            h2c = modulate(rmsnorm(ctx, norm2_g[i]), sh2c, sc2c)
            ctx = ctx + gt2c * swiglu(h2c, w_ffn_gate[i], w_ffn_up[i], w_ffn_down[i])
        x = x_next
    return rmsnorm(x, final_g)
```

```python
import numpy as np
import concourse.bass as bass
import concourse.mybir as mybir
from concourse.bass_utils import run_bass_kernel_spmd

F32 = mybir.dt.float32
BF16 = mybir.dt.bfloat16
AF = mybir.ActivationFunctionType
ALU = mybir.AluOpType

D = 2048
KC = 16
NT = 8
T = 1024
HALO = 512
NH = 10
FFN = 5632
EPS = 1e-6
SB_BASE = 16512
SB_TOP = 229344


class _Rec:
    def __getattr__(self, name):
        def call(*a, **kw):
            self.c = (name, a, kw)
            return self
        return call


class Prog:
    ENG = ("pe", "act", "dve", "pool", "sp")

    def __init__(self):
        self.streams = {e: [] for e in self.ENG}
        self.cnt = {e: 0 for e in self.ENG}
        self.waited = {e: {} for e in self.ENG}
        self.state = {}
        self.dslots = {}

    def _need(self, eng, sem, val, out):
        if self.waited[eng].get(sem, 0) >= val:
            return
        if out.get(sem, 0) < val:
            out[sem] = val

    def op(self, eng, fn, reads=(), writes=(), mark=True, slot=None):
        rec = _Rec()
        fn(rec)
        fn = rec.c
        need = {}
        own = ("E", eng)
        for k in reads:
            st = self.state.get(k)
            if st and st[0] is not None:
                self._need(eng, st[0][0], st[0][1], need)
            if st and isinstance(k, tuple) and k[0] == "psf":
                for sem, val in st[1].items():
                    if sem != own:
                        self._need(eng, sem, val, need)
        for k in writes:
            st = self.state.get(k)
            if st:
                if st[0] is not None:
                    self._need(eng, st[0][0], st[0][1], need)
                for sem, val in st[1].items():
                    self._need(eng, sem, val, need)
        if eng == "pe":
            need.pop(own, None)
        for sem, val in need.items():
            self.streams[eng].append(("w", sem, val))
            self.waited[eng][sem] = val
        if slot is not None:
            self.dslots[slot] = self.dslots.get(slot, 0) + 16
            tok = (("D", slot), self.dslots[slot])
            self.streams[eng].append(("o", fn, tok[0], 16))
        elif mark:
            self.cnt[eng] += 1
            tok = (own, self.cnt[eng])
            self.streams[eng].append(("o", fn, own, 1))
        else:
            assert eng == "pe"
            tok = (own, self.cnt[eng] + 1)
            self.streams[eng].append(("o", fn, None, 0))
        for k in reads:
            st = self.state.setdefault(k, [None, {}])
            if st[1].get(tok[0], 0) < tok[1]:
                st[1][tok[0]] = tok[1]
        for k in writes:
            self.state[k] = [tok, {}]
        return tok

    def retoken(self, keys, slot):
        tok = (("D", slot), self.dslots[slot])
        for k in keys:
            self.state[k] = [tok, {}]

    def barrier(self):
        toks = [(("E", e), self.cnt[e]) for e in self.ENG if self.cnt[e] > 0]
        toks += [(("D", s), v) for s, v in self.dslots.items() if not s.startswith("slab")]
        for e in self.ENG:
            for sem, val in toks:
                if sem == ("E", e) and e == "pe":
                    continue
                if self.waited[e].get(sem, 0) < val:
                    self.streams[e].append(("w", sem, val))
                    self.waited[e][sem] = val
        self.state = {k: v for k, v in self.state.items() if isinstance(k, tuple) and k[0] == "slab"}

    def finish(self, slots):
        for s in slots:
            v = self.dslots[s]
            self.streams["sp"].append(("w", ("D", s), v))

    def emit(self, nc, stack):
        sems = {}

        def semof(key):
            if key not in sems:
                nm = "s_" + "_".join(str(x) for x in key)
                sems[key] = stack.enter_context(nc.semaphore(nm))
            return sems[key]

        for e in self.ENG:
            for it in self.streams[e]:
                if it[0] == "w":
                    semof(it[1])
                elif it[2] is not None:
                    semof(it[2])
        block = stack.enter_context(nc.Block())

        def runner(e):
            def f(engine):
                for it in self.streams[e]:
                    if it[0] == "w":
                        engine.wait_ge(sems[it[1]], it[2])
                    else:
                        nm, a, kw = it[1]
                        ins = getattr(engine, nm)(*a, **kw)
                        if it[2] is not None:
                            ins.then_inc(sems[it[2]], it[3])
            return f

        block.tensor(runner("pe"))
        block.scalar(runner("act"))
        block.vector(runner("dve"))
        block.gpsimd(runner("pool"))
        block.sync(runner("sp"))


def build(dbg=None):
    import contextlib
    nc = bass.Bass("TRN2", target_bir_lowering=False)
    p = Prog()

    def din(name, shape):
        return nc.dram_tensor(name, list(shape), F32, kind="ExternalInput").ap()

    xs = din("xs", [18, 128, D])
    consts_d = din("consts", [128, 5, 128])
    sel_d = din("sel", [2, 128])
    flags_d = din("flags", [128, 2])
    c2_d = din("c2", [32, 128])
    ng_d = din("ng", [32, 128])
    cw_d = din("cw", [24, 128])
    wmod = din("w_mod", [D, 6 * D])
    bmod = din("b_mod", [1, 6 * D])
    win = din("w_in", [D, 6176])
    wg_d = {"f": din("w_gate_f", [16, 512]), "b": din("w_gate_b", [16, 512])}
    bg_d = {"f": din("b_gate_f", [1, 512]), "b": din("b_gate_b", [1, 512])}
    glag_d = din("gla_norm_g", [256])
    wout = din("w_out", [D, D])
    wfg = din("w_ffn_gate", [D, FFN])
    wfu = din("w_ffn_up", [D, FFN])
    wfd = din("w_ffn_down", [FFN, D])
    fg_d = din("final_g", [D])
    out_d = nc.dram_tensor("out", [NT, 128, D], F32, kind="ExternalOutput").ap()

    cur = [SB_BASE]

    def alloc(name, shape, dt, off=None):
        esz = 4 if dt == F32 else 2
        n = 1
        for s in shape[1:]:
            n *= s
        nbytes = (n * esz + 31) // 32 * 32
        if off is None:
            o = cur[0]
            cur[0] += nbytes
        else:
            o = off
        assert o + nbytes <= SB_TOP, (name, o, nbytes)
        return nc.alloc_sbuf_tensor_at(name, list(shape), dt, offset=o)

    NSLAB = 3
    slabs = [alloc(f"slab{i}", [128, KC, 512], BF16) for i in range(NSLAB)]
    consts = alloc("consts_sb", [128, 5, 128], F32)
    identb = alloc("identb", [128, 128], BF16)
    modT = alloc("modT", [128, 4, 16, 2], F32)
    AB = alloc("AB", [128, 6, 16], F32)
    gT = alloc("gT", [128, 32], F32)
    cwT = alloc("cwT", [128, 24], F32)
    cT = alloc("cT", [128, 32], F32)
    scT = alloc("scT", [128, 16, 2], BF16)
    glagb = alloc("glagb", [128, 256], F32)
    bc_off = cur[0]
    bc = alloc("bc", [128, D], F32)
    junkbc = alloc("junkbc", [128, D], BF16, off=bc_off)
    wga = {d: alloc(f"wga_{d}", [33, 512], BF16) for d in "fb"}
    laug = {d: alloc(f"laug_{d}", [33, 1280], BF16) for d in "fb"}
    Sst = {d: alloc(f"S_{d}", [128, 4, 256], F32) for d in "fb"}
    Sfb = alloc("Sfb", [128, 4, 256], BF16)
    flags = alloc("flags_sb", [128, 2], F32)
    ones = alloc("ones", [128, 8], F32)
    sel = alloc("sel_sb", [2, 128], F32)
    mrow = alloc("mrow", [2, 512], F32)
    bm = alloc("bm", [2, 512], F32)
    stat = alloc("stat", [128, 64], F32)
    PH = cur[0]
    PHASE = SB_TOP - PH
    assert PHASE >= 128000, PHASE

    def palloc(name, shape, dt, rel):
        return alloc(name, shape, dt, off=PH + rel)

    rows_c = palloc("rows_c", [32, 128], F32, 0)
    rows_g = palloc("rows_g", [32, 128], F32, 512)
    rows_w = palloc("rows_w", [24, 128], F32, 1024)

    NPS = 8
    psf = [nc.alloc_psum_tensor(f"psf{i}", [128, 512], F32) for i in range(NPS)]
    rr = {"f": 0}

    rr["n"] = 8

    def pf():
        i = rr["f"] % rr["n"]
        rr["f"] = (i + 1) % rr["n"]
        return psf[i], ("psf", i)

    def pb():
        i = rr["f"] % rr["n"]
        rr["f"] = (i + 1) % rr["n"]
        return psf[i][:].bitcast(BF16), ("psf", i)

    ev = {"i": 0}

    def evac_eng():
        ev["i"] ^= 1
        return "act" if ev["i"] else "dve"

    def copy_op(eng, dst, src, reads, writes, scale=None):
        if eng == "act":
            if scale is None:
                p.op("act", lambda e: e.activation(out=dst, in_=src, func=AF.Copy), reads, writes)
            else:
                p.op("act", lambda e: e.activation(out=dst, in_=src, func=AF.Copy, scale=scale), reads, writes)
        else:
            if scale is None:
                p.op(eng, lambda e: e.tensor_copy(out=dst, in_=src), reads, writes)
            else:
                p.op(eng, lambda e: e.tensor_scalar(out=dst, in0=src, scalar1=scale, scalar2=None, op0=ALU.mult),
                     reads, writes)

    jobs = []

    def add_job(src, nk, cols, fn):
        jobs.append((src, nk, cols, fn))

    def issue_slab(i):
        if i >= len(jobs):
            return
        src, nk, cols, _ = jobs[i]
        b = i % NSLAB
        dst = slabs[b][:, 0:nk, 0:cols]
        srcv = src.rearrange("(k p) c -> p k c", p=128)
        p.op("pool", lambda e: e.dma_start(out=dst, in_=srcv), [], [("slab", b)], slot=f"slab{b}")

    def early_slabs():
        for i in range(NSLAB):
            src = wmod[:, i * 512:(i + 1) * 512].rearrange("(k p) c -> p k c", p=128)
            p.op("pool", lambda e: e.dma_start(out=slabs[i][:, :, :], in_=src), [], [("slab", i)], slot=f"slab{i}")

    def run_jobs():
        for i, (_, nk, cols, fn) in enumerate(jobs):
            fn(slabs[i % NSLAB], ("slab", i % NSLAB))
            issue_slab(i + NSLAB)

    early_slabs()

    def sp_load(dst, src, key, slot="small"):
        p.op("sp", lambda e: e.dma_start(out=dst, in_=src), [], [key], slot=slot)

    smallkeys = []
    for dst, src, key in [
        (consts[:], consts_d, "consts"), (sel[:], sel_d, "sel"), (flags[:], flags_d, "flags"),
        (rows_c[:], c2_d, "rows_c"), (rows_g[:], ng_d, "rows_g"), (rows_w[:], cw_d, "rows_w"),
        (glagb[:], glag_d.partition_broadcast(128), "glagb"),
    ]:
        sp_load(dst, src, key)
        smallkeys.append(key)
    p.retoken(smallkeys, "small")
    p.op("pool", lambda e: e.dma_start(out=identb[:], in_=consts_d[:, 0, :]), [], ["identb"], slot="identb")
    p.op("dve", lambda e: e.memset(ones[:], 1.0), [], ["ones"])
    for d in "fb":
        p.op("dve", lambda e, d=d: e.memset(wga[d][:], 0.0), [], [("wga", d)])
        p.op("dve", lambda e, d=d: e.memset(laug[d][:], 1.0), [], [("laug", d)])
        p.op("pool", lambda e, d=d: e.dma_start(out=wga[d][0:16, :], in_=wg_d[d]), [], [("wga", d)], slot=f"wga{d}")
        p.op("pool", lambda e, d=d: e.dma_start(out=wga[d][32:33, :], in_=bg_d[d]), [], [("wga", d)], slot=f"wga{d}")

    ident = consts[:, 0, :]
    TRI = {"f": consts[:, 1, :], "b": consts[:, 2, :]}
    STR = {"f": consts[:, 3, :], "b": consts[:, 4, :]}

    ps, pk = pf()
    p.op("pe", lambda e: e.matmul(ps[:, 0:32], lhsT=rows_c[0:32, :], rhs=consts[0:32, 0, 0:32], start=True, stop=True),
         ["rows_c", "consts"], [pk])
    p.op("act", lambda e: e.activation(out=cT[:], in_=ps[:, 0:32], func=AF.Silu), [pk], ["cT"])
    p.op("dve", lambda e: e.tensor_copy(out=scT[:, :, 0], in_=cT[:, 0:16]), ["cT"], ["scT"])
    p.op("dve", lambda e: e.tensor_copy(out=scT[:, :, 1], in_=cT[:, 16:32]), ["cT"], ["scT"])
    ps2, pk2 = pf()
    p.op("pe", lambda e: e.matmul(ps2[:, 0:32], lhsT=rows_g[0:32, :], rhs=consts[0:32, 0, 0:32], start=True, stop=True),
         ["rows_g", "consts"], [pk2])
    p.op("dve", lambda e: e.tensor_copy(out=gT[:], in_=ps2[:, 0:32]), [pk2], ["gT"])
    ps3, pk3 = pf()
    p.op("pe", lambda e: e.matmul(ps3[:, 0:24], lhsT=rows_w[0:24, :], rhs=consts[0:24, 0, 0:24], start=True, stop=True),
         ["rows_w", "consts"], [pk3])
    p.op("dve", lambda e: e.tensor_copy(out=cwT[:], in_=ps3[:, 0:24]), [pk3], ["cwT"])

    mod_tail = [None]

    def flush_mod_tail():
        if mod_tail[0] is not None:
            f = mod_tail[0]
            mod_tail[0] = None
            f()

    def mod_job(n):
        v = n // 4
        q4 = n % 4

        def fn(slab, skey):
            flush_mod_tail()
            for r in range(2):
                p.op("sp", lambda e, r=r: e.dma_start(out=bm[r:r + 1, :], in_=bmod[:, n * 512:(n + 1) * 512]),
                     [], ["bm"], slot="bm")
            ps, pk = pf()
            for k in range(KC):
                p.op("pe", lambda e, k=k: e.matmul(ps[0:2, :], lhsT=scT[:, k, :], rhs=slab[:, k, :],
                                                    start=(k == 0), stop=(k == KC - 1)),
                     ["scT", skey], [pk], mark=(k == KC - 1))
            p.op("dve", lambda e: e.tensor_tensor(out=mrow[:], in0=ps[0:2, :], in1=bm[:], op=ALU.add),
                 [pk, "bm"], ["mrow"])

            def tail():
                if v in (2, 5):
                    psx, pkx = pf()
                    p.op("pe", lambda e: e.matmul(psx[:], lhsT=sel[0:2, :], rhs=mrow[0:2, :], start=True, stop=True),
                         ["sel", "mrow"], [pkx])
                    p.op("act", lambda e: e.activation(out=bc[:, q4 * 512:(q4 + 1) * 512], in_=psx[:], func=AF.Copy),
                         [pkx], [("bc", q4)])
                else:
                    vi = {0: 0, 1: 1, 3: 2, 4: 3}[v]
                    psx, pkx = pf()
                    for j in range(4):
                        p.op("pe", lambda e, j=j: e.matmul(psx[:, 2 * j:2 * j + 2], lhsT=mrow[0:2, j * 128:(j + 1) * 128],
                                                            rhs=consts[0:2, 0, 0:2], start=True, stop=True),
                             ["mrow", "consts"], [pkx], mark=(j == 3))
                    p.op("dve", lambda e: e.tensor_copy(
                        out=modT[:, vi, q4 * 4:(q4 + 1) * 4, :],
                        in_=psx[:, 0:8].rearrange("p (j t) -> p j t", t=2)), [pkx], [("modT", vi)])
            mod_tail[0] = tail
        return fn

    def finish_mod1():
        flush_mod_tail()
        for j, ab in ((0, 0), (1, 2)):
            p.op("dve", lambda e, j=j, ab=ab: e.scalar_tensor_tensor(
                out=AB[:, ab, :], in0=modT[:, 1, :, j], scalar=1.0, in1=gT[:, 0:16], op0=ALU.add, op1=ALU.mult),
                [("modT", 1), "gT"], [("AB", ab)])
            p.op("dve", lambda e, j=j, ab=ab: e.tensor_copy(out=AB[:, ab + 1, :], in_=modT[:, 0, :, j]),
                 [("modT", 0)], [("AB", ab + 1)])

    def finish_mod2():
        flush_mod_tail()
        p.op("dve", lambda e: e.scalar_tensor_tensor(
            out=AB[:, 4, :], in0=modT[:, 3, :, 0], scalar=1.0, in1=gT[:, 16:32], op0=ALU.add, op1=ALU.mult),
            [("modT", 3), "gT"], [("AB", 4)])
        p.op("dve", lambda e: e.tensor_copy(out=AB[:, 5, :], in_=modT[:, 2, :, 0]), [("modT", 2)], [("AB", 5)])

    class Prep:
        def __init__(self, tiles, xn_bufs, xnname, junk, junkkey, xt_bufs=None, xtname=None):
            self.tiles = tiles
            self.xn = xn_bufs
            self.xnname = xnname
            self.junk = junk
            self.junkkey = junkkey
            self.xt = xt_bufs
            self.xtname = xtname

        def src(self, i):
            tl = self.tiles[i]
            if self.xt is not None:
                b = i % len(self.xt)
                return self.xt[b][:], [(self.xtname, b)]
            return tl["src"], tl["srckeys"]

        def load(self, i):
            if self.xt is None or i >= len(self.tiles):
                return
            b = i % len(self.xt)
            ti = self.tiles[i]["dram"]
            p.op("sp", lambda e: e.dma_start(out=self.xt[b][:], in_=xs[ti]), [], [(self.xtname, b)], slot=f"{self.xtname}{b}")

        def A(self, i):
            src, sk = self.src(i)
            c0 = 2 * (i % 4)
            p.op("act", lambda e: e.activation(out=self.junk[:], in_=src, func=AF.Square, accum_out=stat[:, c0:c0 + 1]),
                 sk, [self.junkkey, ("pst", c0)])

        def B(self, i):
            src, sk = self.src(i)
            c0 = 2 * (i % 4)
            b = i % len(self.xn)
            rs = stat[:, c0 + 1:c0 + 2]
            p.op("act", lambda e: e.activation(out=rs, in_=stat[:, c0:c0 + 1], func=AF.Ln, scale=1.0 / D, bias=EPS),
                 [("pst", c0)], [("pst", c0 + 1)])
            p.op("act", lambda e: e.activation(out=rs, in_=rs, func=AF.Exp, scale=-0.5), [("pst", c0 + 1)], [("pst", c0 + 1)])
            p.op("dve", lambda e: e.tensor_scalar(out=self.xn[b][:], in0=src, scalar1=rs, scalar2=None, op0=ALU.mult),
                 sk + [("pst", c0 + 1)], [(self.xnname, b)])

        def C(self, i):
            tl = self.tiles[i]
            b = i % len(self.xn)
            xn = self.xn[b]
            ai = tl["ai"]
            dst_of_k = tl["dst_of_k"]
            dstname = tl["dstkey"]
            extra_w = tl.get("extra_w", [])
            keyfn = tl.get("dstkeyfn", lambda k: (dstname, k))
            for g in range(2):
                pt, ptk = pb()
                for j in range(8):
                    k = g * 8 + j
                    p.op("pe", lambda e: e.transpose(out=pt[:, j * 128:(j + 1) * 128],
                                                     in_=xn[:, k * 128:(k + 1) * 128], identity=identb[:]),
                         [(self.xnname, b), "identb"], [ptk], mark=(j == 7))
                for j in range(8):
                    k = g * 8 + j
                    if g == 0:
                        p.op("act", lambda e: e.activation(
                            out=dst_of_k(k), in_=pt[:, j * 128:(j + 1) * 128], func=AF.Identity,
                            scale=AB[:, ai, k:k + 1], bias=AB[:, ai + 1, k:k + 1]),
                            [ptk, ("AB", ai), ("AB", ai + 1)], [keyfn(k)] + extra_w)
                    else:
                        p.op("dve", lambda e: e.tensor_scalar(
                            out=dst_of_k(k), in0=pt[:, j * 128:(j + 1) * 128],
                            scalar1=AB[:, ai, k:k + 1], scalar2=AB[:, ai + 1, k:k + 1], op0=ALU.mult, op1=ALU.add),
                            [ptk, ("AB", ai), ("AB", ai + 1)], [keyfn(k)] + extra_w)

        def steps(self):
            n = len(self.tiles)
            self.load(0)
            self.load(1)
            for i in range(n + 2):
                if 0 <= i - 2 < n:
                    self.C(i - 2)
                    yield
                if 0 <= i - 1 < n:
                    self.B(i - 1)
                    if i + 1 >= 2:
                        self.load(i + 1)
                    yield
                if i < n:
                    self.A(i)
                    yield

        def run_skewed(self):
            for _ in self.steps():
                pass

    def gate_sp(d, tok0, e_t, sp_t, spkey, ekey="e_t"):
        ps, pk = pf()
        p.op("pe", lambda e: e.matmul(ps[:], lhsT=laug[d][0:33, tok0:tok0 + 128], rhs=wga[d][0:33, :],
                                      start=True, stop=True), [("laug", d), ("wga", d)], [pk])
        p.op("act", lambda e: e.activation(out=e_t[:], in_=ps[:], func=AF.Exp, scale=-1.0), [pk], [ekey])
        p.op("act", lambda e: e.activation(out=sp_t[:], in_=e_t[:], func=AF.Ln, bias=1.0), [ekey], [spkey])

    def gate_state(d, sp_t, spkey, ktm_ap, ktmkey, ksc_t, kdec_t, dec_t, deckey, ksckey="ksc", kdeckey="kdec"):
        ps, pk = pf()
        p.op("pe", lambda e: e.matmul(ps[:], lhsT=STR[d], rhs=sp_t[:], start=True, stop=True),
             ["consts", spkey], [pk])
        p.op("act", lambda e: e.activation(out=ksc_t[:], in_=ps[:], func=AF.Exp, scale=-1.0 / 16), [pk], [ksckey])
        p.op("dve", lambda e: e.tensor_tensor(out=kdec_t[:], in0=ktm_ap, in1=ksc_t[:], op=ALU.mult),
             [ksckey, ktmkey], [kdeckey])
        ps2, pk2 = pf()
        for h in range(4):
            p.op("pe", lambda e, h=h: e.matmul(ps2[:, h:h + 1], lhsT=sp_t[:, h * 128:(h + 1) * 128], rhs=ones[:, 0:1],
                                                start=True, stop=True), [spkey, "ones"], [pk2], mark=(h == 3))
        p.op("act", lambda e: e.activation(out=dec_t, in_=ps2[:, 0:4], func=AF.Exp, scale=-1.0 / 16), [pk2], [deckey])

    def ds_matmuls(kdec_t, v_ap_of_h, vkey, kdeckey="kdec"):
        for half in range(2):
            ps, pk = psf[6 + half], ("psf", 6 + half)
            for hh in range(2):
                h = half * 2 + hh
                p.op("pe", lambda e, h=h, hh=hh: e.matmul(ps[:, hh * 256:(hh + 1) * 256],
                                                           lhsT=kdec_t[:, h * 128:(h + 1) * 128], rhs=v_ap_of_h(h),
                                                           start=True, stop=True), [kdeckey, vkey], [pk], mark=(hh == 1))

    def s_update(S, skey, dec_t, deckey, first):
        for half in range(2):
            ps, pk = psf[6 + half], ("psf", 6 + half)
            for hh in range(2):
                h = half * 2 + hh
                if first:
                    p.op("dve", lambda e, h=h, hh=hh: e.tensor_copy(out=S[:, h, :], in_=ps[:, hh * 256:(hh + 1) * 256]),
                         [pk], [(skey, h)])
                else:
                    p.op("dve", lambda e, h=h, hh=hh: e.scalar_tensor_tensor(
                        out=S[:, h, :], in0=S[:, h, :], scalar=dec_t[:, h:h + 1], in1=ps[:, hh * 256:(hh + 1) * 256],
                        op0=ALU.mult, op1=ALU.add), [pk, deckey, (skey, h)], [(skey, h)])

    def state_update(d, S, skey, kdec_t, v_ap_of_h, vkey, dec_t, deckey, first):
        ds_matmuls(kdec_t, v_ap_of_h, vkey)
        s_update(S, skey, dec_t, deckey, first)

    hxh = palloc("hxh", [128, KC, 1280], BF16, 0)
    xt = [palloc(f"xt{i}", [128, D], F32, 40960 + i * 8192) for i in range(2)]
    xnh = [palloc(f"xnh{i}", [128, D], BF16, 61440 + i * 4096) for i in range(NH)]
    ktm_h = palloc("ktm_h", [128, NH, 512], BF16, 65536)
    vtm_h = palloc("vtm_h", [128, NH, 1024], BF16, 75776)
    tA = 96256
    e_t = palloc("e_t", [128, 512], F32, tA)
    sp_t = palloc("sp_t", [128, 512], F32, tA + 2048)
    ksc_t = palloc("ksc_t", [128, 512], F32, tA + 4096)
    kdec_t = palloc("kdec_t", [128, 512], BF16, tA + 6144)
    Sh = [palloc(f"Sh{i}", [128, 4, 256], F32, 110592 + i * 4096) for i in range(4)]

    horder = list(range(8, 18))
    htiles = [dict(dram=ti, ai=(2 if ti >= 16 else 0),
                   dst_of_k=(lambda k, j=j: hxh[:, k, j * 128:(j + 1) * 128]), dstkey="hxh")
              for j, ti in enumerate(horder)]
    hprep = Prep(htiles, xnh, "xnh", junkbc, "junkbc", xt_bufs=xt, xtname="xt")
    hstate = {"i": 0}

    def halo_ab(cnt):
        for _ in range(cnt):
            i = hstate["i"]
            if i >= NH:
                return
            if i == 0:
                hprep.load(0)
                hprep.load(1)
            hprep.A(i)
            hprep.B(i)
            hprep.load(i + 2)
            hstate["i"] += 1

    def mod1_job(n):
        mj = mod_job(n)

        def fn(slab, skey):
            mj(slab, skey)
            halo_ab(2 if n < 2 else 1)
        return fn
    for n in range(8):
        add_job(wmod[:, n * 512:(n + 1) * 512], KC, 512, mod1_job(n))

    def halo_prep():
        halo_ab(NH)
        p.barrier()
        finish_mod1()
        for i in range(NH):
            hprep.C(i)
        p.barrier()

    def halo_l_job(slab, skey):
        halo_prep()
        for di, d in enumerate("fb"):
            for (t0, n) in ((0, 512), (512, 512), (1024, 256)):
                ps, pk = pf()
                for k in range(KC):
                    p.op("pe", lambda e, k=k: e.matmul(ps[0:16, 0:n], lhsT=slab[:, k, di * 16:(di + 1) * 16],
                                                        rhs=hxh[:, k, t0:t0 + n], start=(k == 0), stop=(k == KC - 1)),
                         [skey, ("hxh", k)], [pk], mark=(k == KC - 1))
                copy_op(evac_eng(), laug[d][0:16, t0:t0 + n], ps[0:16, 0:n], [pk], [("laug", d)])

    def halo_kv_job(which):
        def fn(slab, skey):
            for j in range(NH):
                ps, pk = pf()
                for k in range(KC):
                    p.op("pe", lambda e, k=k: e.matmul(ps[:], lhsT=hxh[:, k, j * 128:(j + 1) * 128], rhs=slab[:, k, :],
                                                        start=(k == 0), stop=(k == KC - 1)),
                         [("hxh", k), skey], [pk], mark=(k == KC - 1))
                if which == 0:
                    copy_op(evac_eng(), ktm_h[:, j, :], ps[:], [pk], [("ktm_h", j)])
                else:
                    copy_op(evac_eng(), vtm_h[:, j, (which - 1) * 512:which * 512], ps[:], [pk], [("vtm_h", j)])
            if which == 2:
                scan_and_prep()
        return fn

    e_t2 = palloc("e_t2", [128, 512], F32, 103424)
    sp_t2 = palloc("sp_t2", [128, 512], F32, 103424 + 2048)
    ksc_t2 = palloc("ksc_t2", [128, 512], F32, 103424 + 4096)
    kdec_t2 = palloc("kdec_t2", [128, 512], BF16, 103424 + 6144)

    def halo_scans():
        rr["n"] = 6
        segs = [(0, "f", [0, 1, 2, 3]), (1, "b", [7, 6, 5, 4]), (2, "f", [8, 9]), (3, "b", [9, 8])]
        tmps = [(e_t, sp_t, ksc_t, kdec_t, "A", stat[:, 8:12], ("stat", 8)),
                (e_t2, sp_t2, ksc_t2, kdec_t2, "B", stat[:, 28:32], ("stat", 28))]
        for pair in ((segs[0], segs[1]), (segs[2], segs[3])):
            for n in range(len(pair[0][2])):
                for ci, (si, d, tiles) in enumerate(pair):
                    j = tiles[n]
                    et_, spt_, ksct_, kdect_, nm, dect_, deck_ = tmps[ci]
                    gate_sp(d, j * 128, et_, spt_, "sp_t" + nm, ekey="e_t" + nm)
                yield
                for ci, (si, d, tiles) in enumerate(pair):
                    j = tiles[n]
                    et_, spt_, ksct_, kdect_, nm, dect_, deck_ = tmps[ci]
                    gate_state(d, spt_, "sp_t" + nm, ktm_h[:, j, :], ("ktm_h", j), ksct_, kdect_, dect_, deck_,
                               ksckey="ksc" + nm, kdeckey="kdec" + nm)
                yield
                for ci, (si, d, tiles) in enumerate(pair):
                    j = tiles[n]
                    et_, spt_, ksct_, kdect_, nm, dect_, deck_ = tmps[ci]
                    ds_matmuls(kdect_, lambda h, j=j: vtm_h[:, j, h * 256:(h + 1) * 256], ("vtm_h", j),
                               kdeckey="kdec" + nm)
                    s_update(Sh[si], ("Sh", si), dect_, deck_, first=(n == 0))
                yield
        for d, hi, ci, fi in (("f", 0, 2, 0), ("b", 1, 3, 1)):
            for h in range(4):
                p.op("dve", lambda e, h=h, hi=hi, ci=ci: e.tensor_tensor(
                    out=Sh[ci][:, h, :], in0=Sh[ci][:, h, :], in1=Sh[hi][:, h, :], op=ALU.subtract),
                    [(("Sh", ci), h), (("Sh", hi), h)], [(("Sh", ci), h)])
                p.op("dve", lambda e, h=h, hi=hi, ci=ci, d=d, fi=fi: e.scalar_tensor_tensor(
                    out=Sst[d][:, h, :], in0=Sh[ci][:, h, :], scalar=flags[:, fi:fi + 1], in1=Sh[hi][:, h, :],
                    op0=ALU.mult, op1=ALU.add),
                    [(("Sh", ci), h), (("Sh", hi), h), "flags"], [(("S", d), h)])

    add_job(win[:, 3072:3104], KC, 32, halo_l_job)
    add_job(win[:, 512:1024], KC, 512, halo_kv_job(0))
    add_job(win[:, 1024:1536], KC, 512, halo_kv_job(1))
    add_job(win[:, 1536:2048], KC, 512, halo_kv_job(2))

    mixconv = palloc("mixconv", [128, 8, T], BF16, 0)
    hxT = palloc("hxT", [128, KC, T], BF16, 16384)
    mixgla = palloc("mixgla", [128, 8, T], BF16, 16384)
    Sprevb = palloc("Sprevb", [128, NT, 4, 256], BF16, 32768)
    R1 = 49152
    xt2 = [palloc(f"xt2_{i}", [128, D], F32, R1 + i * 8192) for i in range(2)]
    xn2 = palloc("xn2", [128, D], BF16, R1 + 16384)
    u_t = palloc("u_t", [128, T], F32, R1)
    y_t = palloc("y_t", [128, T], F32, R1 + 4096)
    cb_sb = palloc("cb_sb", [128, 8, T], BF16, 69632)
    cc_sb = palloc("cc_sb", [128, 8, T], BF16, 69632 + 16384)
    v_sb = palloc("v_sb", [128, NT, 1024], BF16, 69632)
    sg_sb = palloc("sg_sb", [128, NT, 1024], BF16, 69632 + 16384)
    rawq = palloc("rawq", [128, 4, T], BF16, 102400)
    rawk = palloc("rawk", [128, 4, T], BF16, 102400 + 8192)
    rawktm = palloc("rawktm", [128, NT, 512], BF16, 102400 + 16384)
    g_tmp = palloc("g_tmp", [128, 256], F32, 126976)

    xn2a = palloc("xn2a", [128, D], BF16, 0)
    xn2b = palloc("xn2b", [128, D], BF16, 4096)

    def scan_and_prep():
        p.barrier()
        otiles = [dict(dram=t, ai=0, dst_of_k=(lambda k, t=t: hxT[:, k, t * 128:(t + 1) * 128]), dstkey="hxT")
                  for t in range(NT)]
        pit = Prep(otiles, [xn2a, xn2b], "xn2", junkbc, "junkbc", xt_bufs=xt2, xtname="xt2").steps()
        sit = halo_scans()
        alive = True
        while alive:
            alive = False
            try:
                next(sit)
                alive = True
            except StopIteration:
                pass
            for _ in range(2):
                try:
                    next(pit)
                    alive = True
                except StopIteration:
                    pass

    def own_prep():
        rr["n"] = 8
        p.barrier()

    def own_l_job(slab, skey):
        own_prep()
        for di, d in enumerate("fb"):
            for half in range(2):
                ps, pk = pf()
                for k in range(KC):
                    p.op("pe", lambda e, k=k: e.matmul(ps[0:16, :], lhsT=slab[:, k, di * 16:(di + 1) * 16],
                                                        rhs=hxT[:, k, half * 512:(half + 1) * 512],
                                                        start=(k == 0), stop=(k == KC - 1)),
                         [skey, ("hxT", k)], [pk], mark=(k == KC - 1))
                copy_op(evac_eng(), laug[d][0:16, half * 512:(half + 1) * 512], ps[0:16, :], [pk], [("laug", d)])

    def fm_job(evac):
        def fn(slab, skey):
            for c in range(4):
                for half in range(2):
                    ps, pk = pf()
                    for k in range(KC):
                        p.op("pe", lambda e, k=k: e.matmul(ps[:], lhsT=slab[:, k, c * 128:(c + 1) * 128],
                                                            rhs=hxT[:, k, half * 512:(half + 1) * 512],
                                                            start=(k == 0), stop=(k == KC - 1)),
                             [skey, ("hxT", k)], [pk], mark=(k == KC - 1))
                    evac(c, half, ps, pk)
        return fn

    def tm_job(evac):
        def fn(slab, skey):
            for t in range(NT):
                ps, pk = pf()
                for k in range(KC):
                    p.op("pe", lambda e, k=k: e.matmul(ps[:], lhsT=hxT[:, k, t * 128:(t + 1) * 128], rhs=slab[:, k, :],
                                                        start=(k == 0), stop=(k == KC - 1)),
                         [("hxT", k), skey], [pk], mark=(k == KC - 1))
                evac(t, ps, pk)
        return fn

    def ev_cb(s):
        def f(c, half, ps, pk):
            copy_op(evac_eng(), cb_sb[:, s * 4 + c, half * 512:(half + 1) * 512], ps[:], [pk], [("cb", s * 4 + c)])
        return f

    def ev_cc(s):
        def f(c, half, ps, pk):
            copy_op(evac_eng(), cc_sb[:, s * 4 + c, half * 512:(half + 1) * 512], ps[:], [pk], [("cc", s * 4 + c)])
        return f

    def ev_cx(s):
        def f(c, half, ps, pk):
            ch = s * 4 + c
            p.op("dve", lambda e: e.tensor_tensor(out=u_t[:, half * 512:(half + 1) * 512], in0=ps[:],
                                                  in1=cc_sb[:, ch, half * 512:(half + 1) * 512], op=ALU.mult),
                 [pk, ("cc", ch)], [("u", half)])
            if half == 1:
                u3 = u_t[:].rearrange("p (r w) -> p r w", w=64)
                y3 = y_t[:].rearrange("p (r w) -> p r w", w=64)
                p.op("act", lambda e: e.activation(out=y_t[:], in_=u_t[:], func=AF.Copy, scale=cwT[:, 8 + ch:9 + ch]),
                     [("u", 0), ("u", 1), "cwT"], ["y"])
                p.op("dve", lambda e: e.scalar_tensor_tensor(out=y3[:, :, 1:64], in0=u3[:, :, 0:63],
                                                             scalar=cwT[:, ch:ch + 1], in1=y3[:, :, 1:64],
                                                             op0=ALU.mult, op1=ALU.add),
                     [("u", 0), ("u", 1), "cwT", "y"], ["y"])
                p.op("dve", lambda e: e.scalar_tensor_tensor(out=y3[:, :, 0:63], in0=u3[:, :, 1:64],
                                                             scalar=cwT[:, 16 + ch:17 + ch], in1=y3[:, :, 0:63],
                                                             op0=ALU.mult, op1=ALU.add),
                     [("u", 0), ("u", 1), "cwT", "y"], ["y"])
                p.op("dve", lambda e: e.tensor_tensor(out=mixconv[:, ch, :], in0=y_t[:], in1=cb_sb[:, ch, :], op=ALU.mult),
                     ["y", ("cb", ch)], [("mixconv", ch)])
        return f

    def ev_q(c, half, ps, pk):
        copy_op(evac_eng(), rawq[:, c, half * 512:(half + 1) * 512], ps[:], [pk], [("rawq", c, half)], scale=128.0 ** -0.5)

    def ev_k(c, half, ps, pk):
        copy_op(evac_eng(), rawk[:, c, half * 512:(half + 1) * 512], ps[:], [pk], [("rawk", c, half)])

    def ev_ktm(t, ps, pk):
        copy_op(evac_eng(), rawktm[:, t, :], ps[:], [pk], [("rawktm", t)])

    def ev_v(s):
        def f(t, ps, pk):
            copy_op(evac_eng(), v_sb[:, t, s * 512:(s + 1) * 512], ps[:], [pk], [("v", t)])
        return f

    def ev_g(s):
        def f(t, ps, pk):
            p.op("act", lambda e: e.activation(out=sg_sb[:, t, s * 512:(s + 1) * 512], in_=ps[:], func=AF.Silu),
                 [pk], [("sg", t)])
        return f

    add_job(win[:, 3072:3104], KC, 32, own_l_job)
    mod_mid = list(range(8, 19))

    def add_mod_mid():
        if mod_mid:
            n = mod_mid.pop(0)
            add_job(wmod[:, n * 512:(n + 1) * 512], KC, 512, mod_job(n))
    for s in range(2):
        add_job(win[:, 3104 + s * 512:3104 + (s + 1) * 512], KC, 512, fm_job(ev_cb(s)))
        add_mod_mid()
    for s in range(2):
        add_job(win[:, 4128 + s * 512:4128 + (s + 1) * 512], KC, 512, fm_job(ev_cc(s)))
        add_mod_mid()
    for s in range(2):
        add_job(win[:, 5152 + s * 512:5152 + (s + 1) * 512], KC, 512, fm_job(ev_cx(s)))
        add_mod_mid()
    add_job(win[:, 0:512], KC, 512, fm_job(ev_q))
    add_mod_mid()
    def k_job(slab, skey):
        fm_job(ev_k)(slab, skey)
        for t in range(NT):
            pt, ptk = pb()
            for h in range(4):
                p.op("pe", lambda e: e.transpose(out=pt[:, h * 128:(h + 1) * 128], in_=rawk[:, h, t * 128:(t + 1) * 128],
                                                 identity=identb[:]),
                     [("rawk", h, t // 4), "identb"], [ptk], mark=(h == 3))
            copy_op(evac_eng(), rawktm[:, t, :], pt[:, 0:512], [ptk], [("rawktm", t)])
    add_job(win[:, 512:1024], KC, 512, k_job)
    add_mod_mid()

    def v_job0(slab, skey):
        p.barrier()
        tm_job(ev_v(0))(slab, skey)
    add_job(win[:, 1024:1536], KC, 512, v_job0)
    add_mod_mid()
    add_job(win[:, 1536:2048], KC, 512, tm_job(ev_v(1)))
    add_mod_mid()
    add_job(win[:, 2048:2560], KC, 512, tm_job(ev_g(0)))
    add_mod_mid()

    MASK = TRI

    GJ = {"slab_off": None}

    def gla_alloc():
        so = GJ["slab_off"]
        G = {}
        for ci in range(2):
            G["e", ci] = palloc(f"g_e{ci}", [128, 512], F32, R1 + ci * 2048)
            G["sp", ci] = palloc(f"g_sp{ci}", [128, 512], F32, R1 + 4096 + ci * 2048)
            G["E1", ci] = palloc(f"g_E1{ci}", [128, 512], F32, R1 + 8192 + ci * 2048)
            G["ksc", ci] = palloc(f"g_ksc{ci}", [128, 512], F32, R1 + 12288 + ci * 2048)
            G["ke", ci] = alloc(f"g_ke{ci}", [128, 4, 128], BF16, off=so + 4096 + ci * 1024)
            G["kdec", ci] = alloc(f"g_kdec{ci}", [128, 512], BF16, off=so + 6144 + ci * 1024)
            for par in range(2):
                G["qe", ci, par] = palloc(f"g_qe{ci}{par}", [128, 4, 128], BF16, R1 + 16384 + (2 * ci + par) * 1024)
                G["sT", ci, par] = alloc(f"g_sT{ci}{par}", [128, 4, 128], BF16, off=so + (2 * ci + par) * 1024)
        G["og"] = alloc("g_og", [128, 1024], BF16, off=so + 8192)
        return G

    def gla():
        flush_mod_tail()
        rr["n"] = 4
        p.barrier()
        G = gla_alloc()
        DIR = ("f", "b")

        tl = list(range(NT - 1, -1, -1))
        for pi in range(0, NT, 2):
            if pi > 0:
                yield
            pair = tl[pi:pi + 2]
            for ci, t in enumerate(pair):
                gate_sp("b", t * 128, G["e", ci], G["sp", ci], ("g_sp", ci), ekey=("g_e", ci))
            for ci, t in enumerate(pair):
                gate_state("b", G["sp", ci], ("g_sp", ci), rawktm[:, t, :], ("rawktm", t), G["ksc", ci], G["kdec", ci],
                           stat[:, 8 + 4 * ci:12 + 4 * ci], ("stat", 8 + 4 * ci),
                           ksckey=("g_ksc", ci), kdeckey=("g_kdec", ci))
            for ci, t in enumerate(pair):
                for h in range(4):
                    eng = "act" if h % 2 == 0 else "dve"
                    copy_op(eng, Sprevb[:, t, h, :], Sst["b"][:, h, :], [(("S", "b"), h)], [("Sprevb", t)])
                ds_matmuls(G["kdec", ci], lambda h, t=t: v_sb[:, t, h * 256:(h + 1) * 256], ("v", t),
                           kdeckey=("g_kdec", ci))
                s_update(Sst["b"], ("S", "b"), stat[:, 8 + 4 * ci:12 + 4 * ci], ("stat", 8 + 4 * ci), first=False)

        def S1(t):
            for ci, d in enumerate(DIR):
                gate_sp(d, t * 128, G["e", ci], G["sp", ci], ("g_sp", ci), ekey=("g_e", ci))

        def S2(t, par):
            banks = []
            for ci, d in enumerate(DIR):
                ps, pk = pf()
                banks.append((ps, pk))
                for h in range(4):
                    p.op("pe", lambda e: e.matmul(ps[:, h * 128:(h + 1) * 128],
                                                  lhsT=G["sp", ci][:, h * 128:(h + 1) * 128], rhs=TRI[d],
                                                  start=True, stop=True),
                         [("g_sp", ci), "consts"], [pk], mark=(h == 3))
            for ci, d in enumerate(DIR):
                ps, pk = banks[ci]
                p.op("act", lambda e: e.activation(out=G["E1", ci][:], in_=ps[:], func=AF.Exp, scale=-1.0 / 16),
                     [pk], [("g_E1", ci)])
                p.op("act", lambda e: e.activation(out=G["e", ci][:], in_=ps[:], func=AF.Exp, scale=1.0 / 16),
                     [pk], [("g_e", ci)])
            for ci, d in enumerate(DIR):
                p.op("dve", lambda e: e.tensor_tensor(
                    out=G["qe", ci, par][:], in0=rawq[:, :, t * 128:(t + 1) * 128],
                    in1=G["E1", ci][:].rearrange("p (h i) -> p h i", h=4), op=ALU.mult),
                    [("g_E1", ci)] + [("rawq", c, t // 4) for c in range(4)], [("qe", ci, par)])
                p.op("dve", lambda e: e.tensor_tensor(
                    out=G["ke", ci][:], in0=rawk[:, :, t * 128:(t + 1) * 128],
                    in1=G["e", ci][:].rearrange("p (h i) -> p h i", h=4), op=ALU.mult),
                    [("g_e", ci)] + [("rawk", c, t // 4) for c in range(4)], [("ke", ci)])

        def S3(t, par, dec_t, deckey):
            psa, pka = pf()
            p.op("pe", lambda e: e.matmul(psa[:], lhsT=STR["f"], rhs=G["sp", 0][:], start=True, stop=True),
                 ["consts", ("g_sp", 0)], [pka])
            pst, pkt = pf()
            for h in range(4):
                p.op("pe", lambda e: e.matmul(pst[:, h:h + 1], lhsT=G["sp", 0][:, h * 128:(h + 1) * 128], rhs=ones[:, 0:1],
                                              start=True, stop=True), [("g_sp", 0), "ones"], [pkt], mark=(h == 3))
            sc = []
            for ci, d in enumerate(DIR):
                ps2, pk2 = pf()
                sc.append((ps2, pk2))
                for h in range(4):
                    p.op("pe", lambda e: e.matmul(ps2[:, h * 128:(h + 1) * 128], lhsT=G["ke", ci][:, h, :],
                                                  rhs=G["qe", ci, par][:, h, :], start=True, stop=True),
                         [("ke", ci), ("qe", ci, par)], [pk2], mark=(h == 3))
            p.op("act", lambda e: e.activation(out=G["ksc", 0][:], in_=psa[:], func=AF.Exp, scale=-1.0 / 16),
                 [pka], [("g_ksc", 0)])
            p.op("act", lambda e: e.activation(out=dec_t, in_=pst[:, 0:4], func=AF.Exp, scale=-1.0 / 16), [pkt], [deckey])
            p.op("dve", lambda e: e.tensor_tensor(out=G["kdec", 0][:], in0=rawktm[:, t, :], in1=G["ksc", 0][:], op=ALU.mult),
                 [("g_ksc", 0), ("rawktm", t)], [("g_kdec", 0)])
            for ci, d in enumerate(DIR):
                ps2, pk2 = sc[ci]
                p.op("dve", lambda e: e.tensor_tensor(
                    out=G["sT", ci, par][:], in0=ps2[:].rearrange("p (h i) -> p h i", h=4),
                    in1=MASK[d].unsqueeze(1).to_broadcast([128, 4, 128]), op=ALU.mult),
                    [pk2, "consts"], [("sT", ci, par)])

        def sfb_copy():
            for h in range(4):
                eng = "act" if h % 2 == 0 else "dve"
                copy_op(eng, Sfb[:, h, :], Sst["f"][:, h, :], [(("S", "f"), h)], [("Sfb", h)])

        def S4(t, par, dec_t, deckey):
            ds_matmuls(G["kdec", 0], lambda h: v_sb[:, t, h * 256:(h + 1) * 256], ("v", t), kdeckey=("g_kdec", 0))
            for half in range(2):
                ps, pk = psf[4 + half], ("psf", 4 + half)
                for hh in range(2):
                    h = half * 2 + hh
                    o_ap = ps[:, hh * 256:(hh + 1) * 256]
                    v_ap = v_sb[:, t, h * 256:(h + 1) * 256]
                    p.op("pe", lambda e: e.matmul(o_ap, lhsT=G["sT", 0, par][:, h, :], rhs=v_ap, start=True, stop=False),
                         [("sT", 0, par), ("v", t)], [pk], mark=False)
                    p.op("pe", lambda e: e.matmul(o_ap, lhsT=G["qe", 0, par][:, h, :], rhs=Sfb[:, h, :],
                                                  start=False, stop=False),
                         [("qe", 0, par), ("Sfb", h)], [pk], mark=False)
                    p.op("pe", lambda e: e.matmul(o_ap, lhsT=G["sT", 1, par][:, h, :], rhs=v_ap, start=False, stop=False),
                         [("sT", 1, par), ("v", t)], [pk], mark=False)
                    p.op("pe", lambda e: e.matmul(o_ap, lhsT=G["qe", 1, par][:, h, :], rhs=Sprevb[:, t, h, :],
                                                  start=False, stop=True),
                         [("qe", 1, par), ("Sprevb", t)], [pk], mark=(hh == 1))
            s_update(Sst["f"], ("S", "f"), dec_t, deckey, first=False)
            if t + 1 < NT:
                sfb_copy()

        def S5a(t):
            g_og = G["og"]
            for h in range(4):
                ps, pk = psf[4 + h // 2], ("psf", 4 + h // 2)
                o_ap = ps[:, (h % 2) * 256:(h % 2 + 1) * 256]
                p.op("act", lambda e: e.activation(out=g_og[:, h * 256:(h + 1) * 256], in_=o_ap,
                                                   func=AF.Square, accum_out=stat[:, 16 + h:17 + h]),
                     [pk], [("og", h), ("stat", 16 + h)])
            p.op("act", lambda e: e.activation(out=stat[:, 20:24], in_=stat[:, 16:20], func=AF.Ln, scale=1.0 / 256, bias=EPS),
                 [("stat", 16 + h) for h in range(4)], [("stat", 20)])
            p.op("act", lambda e: e.activation(out=stat[:, 20:24], in_=stat[:, 20:24], func=AF.Exp, scale=-0.5),
                 [("stat", 20)], [("stat", 20)])
            for h in range(4):
                ps, pk = psf[4 + h // 2], ("psf", 4 + h // 2)
                o_ap = ps[:, (h % 2) * 256:(h % 2 + 1) * 256]
                p.op("dve", lambda e: e.scalar_tensor_tensor(
                    out=g_tmp[:], in0=o_ap, scalar=stat[:, 20 + h:21 + h], in1=glagb[:],
                    op0=ALU.mult, op1=ALU.mult), [pk, ("stat", 20), "glagb"], ["g_tmp"])
                p.op("dve", lambda e: e.tensor_tensor(
                    out=g_og[:, h * 256:(h + 1) * 256], in0=g_tmp[:],
                    in1=sg_sb[:, t, h * 256:(h + 1) * 256], op=ALU.mult), ["g_tmp", ("sg", t)], [("og", h)])

        def S5b(t):
            g_og = G["og"]
            pt, ptk = pb()
            for j in range(8):
                p.op("pe", lambda e: e.transpose(out=pt[:, j * 128:(j + 1) * 128], in_=g_og[:, j * 128:(j + 1) * 128],
                                                 identity=identb[:]), [("og", j // 2), "identb"], [ptk], mark=(j == 7))
            copy_op(evac_eng(), mixgla[:, :, t * 128:(t + 1) * 128], pt[:].rearrange("p (j i) -> p j i", j=8),
                    [ptk], [("mixgla", t)])

        sfb_copy()
        for t in range(NT):
            yield
            par = t % 2
            dec_t = stat[:, 40:44] if par == 0 else stat[:, 44:48]
            deckey = ("stat", 40 if par == 0 else 44)
            S1(t)
            S2(t, par)
            S3(t, par, dec_t, deckey)
            if t > 0:
                S5b(t - 1)
            S4(t, par, dec_t, deckey)
            S5a(t)
        S5b(NT - 1)
        p.barrier()

    gla_it = gla()

    def gla_step(k=1):
        for _ in range(k):
            try:
                next(gla_it)
            except StopIteration:
                pass

    def g_job1(slab, skey):
        tm_job(ev_g(1))(slab, skey)
        gla_step(64)
    GJ["slab_off"] = SB_BASE + (len(jobs) % NSLAB) * (KC * 512 * 2)
    add_job(win[:, 2560:3072], KC, 512, g_job1)

    xres = palloc("xres", [128, NT, D], F32, 32768)
    etmp = [palloc(f"etmp{i}", [128, 512], F32, 98304 + i * 2048) for i in range(2)]
    et = {"i": 0}

    def resid_evac(ps, pk, t, n):
        i = et["i"]
        et["i"] ^= 1
        p.op("dve", lambda e: e.tensor_tensor(out=etmp[i][:], in0=ps[:], in1=bc[:, n * 512:(n + 1) * 512], op=ALU.mult),
             [pk, ("bc", n)], [("etmp", i)])
        p.op("pool", lambda e: e.tensor_tensor(out=xres[:, t, n * 512:(n + 1) * 512], in0=xres[:, t, n * 512:(n + 1) * 512],
                                               in1=etmp[i][:], op=ALU.add),
             [("etmp", i), ("xres", t)], [("xres", t)])

    def outproj_job(n):
        def fn(slab, skey):
            if n == 0:
                rr["n"] = 8
                p.barrier()
                for t in range(NT):
                    p.op("sp", lambda e, t=t: e.dma_start(out=xres[:, t, :], in_=xs[t]), [], [("xres", t)], slot="xres")
                p.retoken([("xres", t) for t in range(NT)], "xres")
            fprep = None
            if n == 3:
                finish_mod2()
                ftiles = [dict(src=xres[:, t, :], srckeys=[("xres", t)], ai=4,
                               dst_of_k=(lambda k, t=t: h2T[:, k, t * 128:(t + 1) * 128]), dstkey="h2T",
                               dstkeyfn=(lambda k, t=t: ("h2T", k, t // 4)),
                               extra_w=[("mixcol", t)]) for t in range(NT)]
                fprep = Prep(ftiles, [xn3a, xn3b], "xn3", junk3, "junk3")
            for t in range(NT):
                ps, pk = pf()
                for k in range(KC):
                    src = mixgla if k < 8 else mixconv
                    p.op("pe", lambda e, k=k, src=src: e.matmul(ps[:], lhsT=src[:, k % 8, t * 128:(t + 1) * 128],
                                                                 rhs=slab[:, k, :], start=(k == 0), stop=(k == KC - 1)),
                         [("mixcol", t), skey], [pk], mark=(k == KC - 1))
                resid_evac(ps, pk, t, n)
                if fprep is not None:
                    if t - 1 >= 0:
                        fprep.A(t - 1)
                    if t - 2 >= 0:
                        fprep.B(t - 2)
                    if t - 3 >= 0:
                        fprep.C(t - 3)
            if fprep is not None:
                fprep.A(NT - 1)
                fprep.B(NT - 2)
                fprep.C(NT - 3)
                fprep.B(NT - 1)
                fprep.C(NT - 2)
                fprep.C(NT - 1)
        return fn
    for n in range(4):
        add_job(wout[:, n * 512:(n + 1) * 512], KC, 512, outproj_job(n))
        if n == 0:
            add_job(wmod[:, 19 * 512:20 * 512], KC, 512, mod_job(19))


    h2T = palloc("h2T", [128, KC, T], BF16, 0)
    hT = palloc("hT", [128, 8, T], BF16, 98304)
    xn3 = palloc("xn3", [128, D], BF16, 98304)
    sgate = palloc("sgate", [128, 4, T], BF16, 114688)
    etmp2 = [palloc(f"etmp2_{i}", [128, 512], F32, 122880 + i * 2048) for i in range(2)]

    xn3a = palloc("xn3a", [128, D], BF16, 102400)
    xn3b = palloc("xn3b", [128, D], BF16, 106496)
    junk3 = palloc("junk3", [128, D], BF16, 110592)

    def ffn_prep():
        p.barrier()

    pieces = [(0, 8), (8, 8), (16, 8), (24, 8), (32, 8), (40, 4)]

    def gate_job(pi, s, first):
        def fn(slab, skey):
            for c in range(4):
                for half in range(2):
                    ps, pk = pf()
                    for k in range(KC):
                        p.op("pe", lambda e, k=k: e.matmul(ps[:], lhsT=slab[:, k, c * 128:(c + 1) * 128],
                                                            rhs=h2T[:, k, half * 512:(half + 1) * 512],
                                                            start=(k == 0), stop=(k == KC - 1)),
                             [skey, ("h2T", k, half)], [pk], mark=(k == KC - 1))
                    p.op("act", lambda e: e.activation(out=sgate[:, c, half * 512:(half + 1) * 512], in_=ps[:], func=AF.Silu),
                         [pk], [("sgate", c, half)])
        return fn

    def up_job(pi, s):
        def fn(slab, skey):
            if pi == 0 and s == 0:
                ffn_prep()
            for c in range(4):
                for half in range(2):
                    ps, pk = pf()
                    for k in range(KC):
                        p.op("pe", lambda e, k=k: e.matmul(ps[:], lhsT=slab[:, k, c * 128:(c + 1) * 128],
                                                            rhs=h2T[:, k, half * 512:(half + 1) * 512],
                                                            start=(k == 0), stop=(k == KC - 1)),
                             [skey, ("h2T", k, half)], [pk], mark=(k == KC - 1))
                    p.op("dve", lambda e: e.tensor_tensor(out=hT[:, s * 4 + c, half * 512:(half + 1) * 512], in0=ps[:],
                                                          in1=sgate[:, c, half * 512:(half + 1) * 512], op=ALU.mult),
                         [pk, ("sgate", c, half)], [("hT", s * 4 + c)])
        return fn

    et2 = {"i": 0}

    def down_job(pi, n, nch):
        def fn(slab, skey):
            flush_mod_tail()
            for t in range(NT):
                ps, pk = pf()
                for c in range(nch):
                    p.op("pe", lambda e, c=c: e.matmul(ps[:], lhsT=hT[:, c, t * 128:(t + 1) * 128], rhs=slab[:, c, :],
                                                        start=(c == 0), stop=(c == nch - 1)),
                         [("hT", c), skey], [pk], mark=(c == nch - 1))
                i = et2["i"]
                et2["i"] ^= 1
                p.op("dve", lambda e, i=i: e.tensor_tensor(out=etmp2[i][:], in0=ps[:], in1=bc[:, n * 512:(n + 1) * 512],
                                                          op=ALU.mult), [pk, ("bc", n)], [("etmp2", i)])
                p.op("pool", lambda e, i=i: e.tensor_tensor(out=xres[:, t, n * 512:(n + 1) * 512],
                                                           in0=xres[:, t, n * 512:(n + 1) * 512], in1=etmp2[i][:], op=ALU.add),
                     [("etmp2", i), ("xres", t)], [("xres", t)])
        return fn

    gt2_jobs = list(range(20, 24))
    for pi, (c0, nch) in enumerate(pieces):
        for s in range(nch // 4):
            col = (c0 + s * 4) * 128
            add_job(wfg[:, col:col + 512], KC, 512, gate_job(pi, s, first=(pi == 0 and s == 0)))
            if gt2_jobs:
                n = gt2_jobs.pop(0)
                add_job(wmod[:, n * 512:(n + 1) * 512], KC, 512, mod_job(n))
            add_job(wfu[:, col:col + 512], KC, 512, up_job(pi, s))
            if gt2_jobs:
                n = gt2_jobs.pop(0)
                add_job(wmod[:, n * 512:(n + 1) * 512], KC, 512, mod_job(n))
        for n in range(4):
            add_job(wfd[c0 * 128:(c0 + nch) * 128, n * 512:(n + 1) * 512], nch, 512, down_job(pi, n, nch))

    run_jobs()

    p.barrier()
    p.op("sp", lambda e: e.dma_start(out=bc[:], in_=fg_d.partition_broadcast(128)), [], ["fgb"], slot="fgb")
    ot = [palloc(f"ot{i}", [128, D], F32, i * 8192) for i in range(2)]
    junk = palloc("junk", [128, D], BF16, 16384)
    for t in range(NT):
        i = t % 2
        ss = stat[:, 32 + 2 * i:33 + 2 * i]
        rs = stat[:, 33 + 2 * i:34 + 2 * i]
        p.op("act", lambda e, ss=ss: e.activation(out=junk[:], in_=xres[:, t, :], func=AF.Square, accum_out=ss),
             [("xres", t)], ["junk", ("fs", i)])
        p.op("act", lambda e, ss=ss, rs=rs: e.activation(out=rs, in_=ss, func=AF.Ln, scale=1.0 / D, bias=EPS),
             [("fs", i)], [("fr", i)])
        p.op("act", lambda e, rs=rs: e.activation(out=rs, in_=rs, func=AF.Exp, scale=-0.5), [("fr", i)], [("fr", i)])
        p.op("dve", lambda e, rs=rs, i=i: e.scalar_tensor_tensor(out=ot[i][:], in0=xres[:, t, :], scalar=rs, in1=bc[:],
                                                                op0=ALU.mult, op1=ALU.mult),
             [("xres", t), ("fr", i), "fgb"], [("ot", i)])
        p.op("sp", lambda e, i=i: e.dma_start(out=out_d[t], in_=ot[i][:]), [("ot", i)], [], slot=f"out{i}")
    p.finish(["out0", "out1"])

    with contextlib.ExitStack() as stack:
        p.emit(nc, stack)
    return nc


def _host_inputs(inputs):
    f = lambda a: np.ascontiguousarray(np.asarray(a, dtype=np.float32))
    x = f(inputs["x"])[0]
    ctx = f(inputs["ctx"])[0]
    idx = np.arange(128)
    consts = np.zeros((128, 5, 128), np.float32)
    consts[:, 0, :] = np.eye(128, dtype=np.float32)
    consts[:, 1, :] = (idx[:, None] <= idx[None, :])
    consts[:, 2, :] = (idx[:, None] >= idx[None, :])
    consts[:, 3, :] = (idx[:, None] > idx[None, :])
    consts[:, 4, :] = (idx[:, None] < idx[None, :])
    sel = np.zeros((2, 128), np.float32)
    sel[0] = 1.0
    common = {
        "consts": consts, "sel": sel,
        "c2": np.concatenate([f(inputs["c"]).reshape(16, 128), f(inputs["c_ctx"]).reshape(16, 128)], 0),
        "ng": np.concatenate([f(inputs["norm1_g"]).reshape(16, 128), f(inputs["norm2_g"]).reshape(16, 128)], 0),
        "cw": f(inputs["conv_w"]).reshape(24, 128),
        "w_mod": f(inputs["w_mod"])[0], "b_mod": f(inputs["b_mod"]).reshape(1, -1),
        "w_in": f(inputs["w_in"])[0],
        "w_gate_f": f(inputs["w_gate_f"])[0], "w_gate_b": f(inputs["w_gate_b"])[0],
        "b_gate_f": f(inputs["b_gate_f"]).reshape(1, 512), "b_gate_b": f(inputs["b_gate_b"]).reshape(1, 512),
        "gla_norm_g": f(inputs["gla_norm_g"]).reshape(256),
        "w_out": f(inputs["w_out"])[0],
        "w_ffn_gate": f(inputs["w_ffn_gate"])[0], "w_ffn_up": f(inputs["w_ffn_up"])[0],
        "w_ffn_down": f(inputs["w_ffn_down"])[0],
        "final_g": f(inputs["final_g"]).reshape(D),
    }
    maps = []
    zeros = np.zeros((HALO, D), np.float32)
    for c in range(8):
        s = c * T
        own = x[s:s + T]
        before = x[s - HALO:s] if c > 0 else zeros
        after = x[s + T:s + T + HALO] if c < 7 else zeros
        xs = np.concatenate([own, before, after, ctx], 0).reshape(18, 128, D)
        flags = np.zeros((128, 2), np.float32)
        flags[:, 0] = 1.0 if c == 0 else 0.0
        flags[:, 1] = 1.0 if c == 7 else 0.0
        m = dict(common)
        m["xs"] = np.ascontiguousarray(xs)
        m["flags"] = flags
        maps.append(m)
    return maps


_NC_CACHE = {}


def kernel(**inputs):
    maps = _host_inputs(inputs)
    if "nc" not in _NC_CACHE:
        _NC_CACHE["nc"] = build()
    nc = _NC_CACHE["nc"]
    res = run_bass_kernel_spmd(nc, maps, core_ids=list(range(8)))
    outs = [np.asarray(r["out"]).reshape(T, D) for r in res.results]
    return np.concatenate(outs, 0).reshape(1, 8 * T, D).astype(np.float32)
```

```python
import numpy as np
import concourse.bass as bass
import concourse.mybir as mybir
from concourse.bass_utils import run_bass_kernel_spmd

F32 = mybir.dt.float32
BF16 = mybir.dt.bfloat16
AF = mybir.ActivationFunctionType
ALU = mybir.AluOpType

D = 2048
KC = 16
NT = 8
T = 1024
HALO = 512
NH = 10
FFN = 5632
EPS = 1e-6
SB_BASE = 16512
SB_TOP = 229344


class _Rec:
    def __getattr__(self, name):
        def call(*a, **kw):
            self.c = (name, a, kw)
            return self
        return call


class Prog:
    ENG = ("pe", "act", "dve", "pool", "sp")

    def __init__(self):
        self.streams = {e: [] for e in self.ENG}
        self.cnt = {e: 0 for e in self.ENG}
        self.waited = {e: {} for e in self.ENG}
        self.state = {}
        self.dslots = {}

    def _need(self, eng, sem, val, out):
        if self.waited[eng].get(sem, 0) >= val:
            return
        if out.get(sem, 0) < val:
            out[sem] = val

    def op(self, eng, fn, reads=(), writes=(), mark=True, slot=None):
        rec = _Rec()
        fn(rec)
        fn = rec.c
        need = {}
        own = ("E", eng)
        for k in reads:
            st = self.state.get(k)
            if st and st[0] is not None:
                self._need(eng, st[0][0], st[0][1], need)
            if st and isinstance(k, tuple) and k[0] == "psf":
                for sem, val in st[1].items():
                    if sem != own:
                        self._need(eng, sem, val, need)
        for k in writes:
            st = self.state.get(k)
            if st:
                if st[0] is not None:
                    self._need(eng, st[0][0], st[0][1], need)
                for sem, val in st[1].items():
                    self._need(eng, sem, val, need)
        if eng == "pe":
            need.pop(own, None)
        for sem, val in need.items():
            self.streams[eng].append(("w", sem, val))
            self.waited[eng][sem] = val
        if slot is not None:
            self.dslots[slot] = self.dslots.get(slot, 0) + 16
            tok = (("D", slot), self.dslots[slot])
            self.streams[eng].append(("o", fn, tok[0], 16))
        elif mark:
            self.cnt[eng] += 1
            tok = (own, self.cnt[eng])
            self.streams[eng].append(("o", fn, own, 1))
        else:
            assert eng == "pe"
            tok = (own, self.cnt[eng] + 1)
            self.streams[eng].append(("o", fn, None, 0))
        for k in reads:
            st = self.state.setdefault(k, [None, {}])
            if st[1].get(tok[0], 0) < tok[1]:
                st[1][tok[0]] = tok[1]
        for k in writes:
            self.state[k] = [tok, {}]
        return tok

    def retoken(self, keys, slot):
        tok = (("D", slot), self.dslots[slot])
        for k in keys:
            self.state[k] = [tok, {}]

    def barrier(self):
        toks = [(("E", e), self.cnt[e]) for e in self.ENG if self.cnt[e] > 0]
        toks += [(("D", s), v) for s, v in self.dslots.items() if not s.startswith("slab")]
        for e in self.ENG:
            for sem, val in toks:
                if sem == ("E", e) and e == "pe":
                    continue
                if self.waited[e].get(sem, 0) < val:
                    self.streams[e].append(("w", sem, val))
                    self.waited[e][sem] = val
        self.state = {k: v for k, v in self.state.items() if isinstance(k, tuple) and k[0] == "slab"}

    def finish(self, slots):
        for s in slots:
            v = self.dslots[s]
            self.streams["sp"].append(("w", ("D", s), v))

    def emit(self, nc, stack):
        sems = {}

        def semof(key):
            if key not in sems:
                nm = "s_" + "_".join(str(x) for x in key)
                sems[key] = stack.enter_context(nc.semaphore(nm))
            return sems[key]

        for e in self.ENG:
            for it in self.streams[e]:
                if it[0] == "w":
                    semof(it[1])
                elif it[2] is not None:
                    semof(it[2])
        block = stack.enter_context(nc.Block())

        def runner(e):
            def f(engine):
                for it in self.streams[e]:
                    if it[0] == "w":
                        engine.wait_ge(sems[it[1]], it[2])
                    else:
                        nm, a, kw = it[1]
                        ins = getattr(engine, nm)(*a, **kw)
                        if it[2] is not None:
                            ins.then_inc(sems[it[2]], it[3])
            return f

        block.tensor(runner("pe"))
        block.scalar(runner("act"))
        block.vector(runner("dve"))
        block.gpsimd(runner("pool"))
        block.sync(runner("sp"))


def build(dbg=None):
    import contextlib
    nc = bass.Bass("TRN2", target_bir_lowering=False)
    p = Prog()

    def din(name, shape):
        return nc.dram_tensor(name, list(shape), F32, kind="ExternalInput").ap()

    xs = din("xs", [18, 128, D])
    consts_d = din("consts", [128, 5, 128])
    sel_d = din("sel", [2, 128])
    flags_d = din("flags", [128, 2])
    c2_d = din("c2", [32, 128])
    ng_d = din("ng", [32, 128])
    cw_d = din("cw", [24, 128])
    wmod = din("w_mod", [D, 6 * D])
    bmod = din("b_mod", [1, 6 * D])
    win = din("w_in", [D, 6176])
    wg_d = {"f": din("w_gate_f", [16, 512]), "b": din("w_gate_b", [16, 512])}
    bg_d = {"f": din("b_gate_f", [1, 512]), "b": din("b_gate_b", [1, 512])}
    glag_d = din("gla_norm_g", [256])
    wout = din("w_out", [D, D])
    wfg = din("w_ffn_gate", [D, FFN])
    wfu = din("w_ffn_up", [D, FFN])
    wfd = din("w_ffn_down", [FFN, D])
    fg_d = din("final_g", [D])
    out_d = nc.dram_tensor("out", [NT, 128, D], F32, kind="ExternalOutput").ap()

    cur = [SB_BASE]

    def alloc(name, shape, dt, off=None):
        esz = 4 if dt == F32 else 2
        n = 1
        for s in shape[1:]:
            n *= s
        nbytes = (n * esz + 31) // 32 * 32
        if off is None:
            o = cur[0]
            cur[0] += nbytes
        else:
            o = off
        assert o + nbytes <= SB_TOP, (name, o, nbytes)
        return nc.alloc_sbuf_tensor_at(name, list(shape), dt, offset=o)

    NSLAB = 3
    slabs = [alloc(f"slab{i}", [128, KC, 512], BF16) for i in range(NSLAB)]
    consts = alloc("consts_sb", [128, 5, 128], F32)
    identb = alloc("identb", [128, 128], BF16)
    modT = alloc("modT", [128, 4, 16, 2], F32)
    AB = alloc("AB", [128, 6, 16], F32)
    gT = alloc("gT", [128, 32], F32)
    cwT = alloc("cwT", [128, 24], F32)
    cT = alloc("cT", [128, 32], F32)
    scT = alloc("scT", [128, 16, 2], BF16)
    glagb = alloc("glagb", [128, 256], F32)
    bc_off = cur[0]
    bc = alloc("bc", [128, D], F32)
    junkbc = alloc("junkbc", [128, D], BF16, off=bc_off)
    wga = {d: alloc(f"wga_{d}", [33, 512], BF16) for d in "fb"}
    laug = {d: alloc(f"laug_{d}", [33, 1280], BF16) for d in "fb"}
    Sst = {d: alloc(f"S_{d}", [128, 4, 256], F32) for d in "fb"}
    Sfb = alloc("Sfb", [128, 4, 256], BF16)
    flags = alloc("flags_sb", [128, 2], F32)
    ones = alloc("ones", [128, 8], F32)
    sel = alloc("sel_sb", [2, 128], F32)
    mrow = alloc("mrow", [2, 512], F32)
    bm = alloc("bm", [2, 512], F32)
    stat = alloc("stat", [128, 64], F32)
    PH = cur[0]
    PHASE = SB_TOP - PH
    assert PHASE >= 128000, PHASE

    def palloc(name, shape, dt, rel):
        return alloc(name, shape, dt, off=PH + rel)

    rows_c = palloc("rows_c", [32, 128], F32, 0)
    rows_g = palloc("rows_g", [32, 128], F32, 512)
    rows_w = palloc("rows_w", [24, 128], F32, 1024)

    NPS = 8
    psf = [nc.alloc_psum_tensor(f"psf{i}", [128, 512], F32) for i in range(NPS)]
    rr = {"f": 0}

    rr["n"] = 8

    def pf():
        i = rr["f"] % rr["n"]
        rr["f"] = (i + 1) % rr["n"]
        return psf[i], ("psf", i)

    def pb():
        i = rr["f"] % rr["n"]
        rr["f"] = (i + 1) % rr["n"]
        return psf[i][:].bitcast(BF16), ("psf", i)

    ev = {"i": 0}

    def evac_eng():
        ev["i"] ^= 1
        return "act" if ev["i"] else "dve"

    def copy_op(eng, dst, src, reads, writes, scale=None):
        if eng == "act":
            if scale is None:
                p.op("act", lambda e: e.activation(out=dst, in_=src, func=AF.Copy), reads, writes)
            else:
                p.op("act", lambda e: e.activation(out=dst, in_=src, func=AF.Copy, scale=scale), reads, writes)
        else:
            if scale is None:
                p.op(eng, lambda e: e.tensor_copy(out=dst, in_=src), reads, writes)
            else:
                p.op(eng, lambda e: e.tensor_scalar(out=dst, in0=src, scalar1=scale, scalar2=None, op0=ALU.mult),
                     reads, writes)

    jobs = []

    def add_job(src, nk, cols, fn):
        jobs.append((src, nk, cols, fn))

    def issue_slab(i):
        if i >= len(jobs):
            return
        src, nk, cols, _ = jobs[i]
        b = i % NSLAB
        dst = slabs[b][:, 0:nk, 0:cols]
        srcv = src.rearrange("(k p) c -> p k c", p=128)
        p.op("pool", lambda e: e.dma_start(out=dst, in_=srcv), [], [("slab", b)], slot=f"slab{b}")

    def early_slabs():
        for i in range(NSLAB):
            src = wmod[:, i * 512:(i + 1) * 512].rearrange("(k p) c -> p k c", p=128)
            p.op("pool", lambda e: e.dma_start(out=slabs[i][:, :, :], in_=src), [], [("slab", i)], slot=f"slab{i}")

    def run_jobs():
        for i, (_, nk, cols, fn) in enumerate(jobs):
            fn(slabs[i % NSLAB], ("slab", i % NSLAB))
            issue_slab(i + NSLAB)

    early_slabs()

    def sp_load(dst, src, key, slot="small"):
        p.op("sp", lambda e: e.dma_start(out=dst, in_=src), [], [key], slot=slot)

    smallkeys = []
    for dst, src, key in [
        (consts[:], consts_d, "consts"), (sel[:], sel_d, "sel"), (flags[:], flags_d, "flags"),
        (rows_c[:], c2_d, "rows_c"), (rows_g[:], ng_d, "rows_g"), (rows_w[:], cw_d, "rows_w"),
        (glagb[:], glag_d.partition_broadcast(128), "glagb"),
    ]:
        sp_load(dst, src, key)
        smallkeys.append(key)
    p.retoken(smallkeys, "small")
    p.op("pool", lambda e: e.dma_start(out=identb[:], in_=consts_d[:, 0, :]), [], ["identb"], slot="identb")
    p.op("dve", lambda e: e.memset(ones[:], 1.0), [], ["ones"])
    for d in "fb":
        p.op("dve", lambda e, d=d: e.memset(wga[d][:], 0.0), [], [("wga", d)])
        p.op("dve", lambda e, d=d: e.memset(laug[d][:], 1.0), [], [("laug", d)])
        p.op("pool", lambda e, d=d: e.dma_start(out=wga[d][0:16, :], in_=wg_d[d]), [], [("wga", d)], slot=f"wga{d}")
        p.op("pool", lambda e, d=d: e.dma_start(out=wga[d][32:33, :], in_=bg_d[d]), [], [("wga", d)], slot=f"wga{d}")

    ident = consts[:, 0, :]
    TRI = {"f": consts[:, 1, :], "b": consts[:, 2, :]}
    STR = {"f": consts[:, 3, :], "b": consts[:, 4, :]}

    ps, pk = pf()
    p.op("pe", lambda e: e.matmul(ps[:, 0:32], lhsT=rows_c[0:32, :], rhs=consts[0:32, 0, 0:32], start=True, stop=True),
         ["rows_c", "consts"], [pk])
    p.op("act", lambda e: e.activation(out=cT[:], in_=ps[:, 0:32], func=AF.Silu), [pk], ["cT"])
    p.op("dve", lambda e: e.tensor_copy(out=scT[:, :, 0], in_=cT[:, 0:16]), ["cT"], ["scT"])
    p.op("dve", lambda e: e.tensor_copy(out=scT[:, :, 1], in_=cT[:, 16:32]), ["cT"], ["scT"])
    ps2, pk2 = pf()
    p.op("pe", lambda e: e.matmul(ps2[:, 0:32], lhsT=rows_g[0:32, :], rhs=consts[0:32, 0, 0:32], start=True, stop=True),
         ["rows_g", "consts"], [pk2])
    p.op("dve", lambda e: e.tensor_copy(out=gT[:], in_=ps2[:, 0:32]), [pk2], ["gT"])
    ps3, pk3 = pf()
    p.op("pe", lambda e: e.matmul(ps3[:, 0:24], lhsT=rows_w[0:24, :], rhs=consts[0:24, 0, 0:24], start=True, stop=True),
         ["rows_w", "consts"], [pk3])
    p.op("dve", lambda e: e.tensor_copy(out=cwT[:], in_=ps3[:, 0:24]), [pk3], ["cwT"])

    mod_tail = [None]

    def flush_mod_tail():
        if mod_tail[0] is not None:
            f = mod_tail[0]
            mod_tail[0] = None
            f()

    def mod_job(n):
        v = n // 4
        q4 = n % 4

        def fn(slab, skey):
            flush_mod_tail()
            for r in range(2):
                p.op("sp", lambda e, r=r: e.dma_start(out=bm[r:r + 1, :], in_=bmod[:, n * 512:(n + 1) * 512]),
                     [], ["bm"], slot="bm")
            ps, pk = pf()
            for k in range(KC):
                p.op("pe", lambda e, k=k: e.matmul(ps[0:2, :], lhsT=scT[:, k, :], rhs=slab[:, k, :],
                                                    start=(k == 0), stop=(k == KC - 1)),
                     ["scT", skey], [pk], mark=(k == KC - 1))
            p.op("dve", lambda e: e.tensor_tensor(out=mrow[:], in0=ps[0:2, :], in1=bm[:], op=ALU.add),
                 [pk, "bm"], ["mrow"])

            def tail():
                if v in (2, 5):
                    psx, pkx = pf()
                    p.op("pe", lambda e: e.matmul(psx[:], lhsT=sel[0:2, :], rhs=mrow[0:2, :], start=True, stop=True),
                         ["sel", "mrow"], [pkx])
                    p.op("act", lambda e: e.activation(out=bc[:, q4 * 512:(q4 + 1) * 512], in_=psx[:], func=AF.Copy),
                         [pkx], [("bc", q4)])
                else:
                    vi = {0: 0, 1: 1, 3: 2, 4: 3}[v]
                    psx, pkx = pf()
                    for j in range(4):
                        p.op("pe", lambda e, j=j: e.matmul(psx[:, 2 * j:2 * j + 2], lhsT=mrow[0:2, j * 128:(j + 1) * 128],
                                                            rhs=consts[0:2, 0, 0:2], start=True, stop=True),
                             ["mrow", "consts"], [pkx], mark=(j == 3))
                    p.op("dve", lambda e: e.tensor_copy(
                        out=modT[:, vi, q4 * 4:(q4 + 1) * 4, :],
                        in_=psx[:, 0:8].rearrange("p (j t) -> p j t", t=2)), [pkx], [("modT", vi)])
            mod_tail[0] = tail
        return fn

    def finish_mod1():
        flush_mod_tail()
        for j, ab in ((0, 0), (1, 2)):
            p.op("dve", lambda e, j=j, ab=ab: e.scalar_tensor_tensor(
                out=AB[:, ab, :], in0=modT[:, 1, :, j], scalar=1.0, in1=gT[:, 0:16], op0=ALU.add, op1=ALU.mult),
                [("modT", 1), "gT"], [("AB", ab)])
            p.op("dve", lambda e, j=j, ab=ab: e.tensor_copy(out=AB[:, ab + 1, :], in_=modT[:, 0, :, j]),
                 [("modT", 0)], [("AB", ab + 1)])

    def finish_mod2():
        flush_mod_tail()
        p.op("dve", lambda e: e.scalar_tensor_tensor(
            out=AB[:, 4, :], in0=modT[:, 3, :, 0], scalar=1.0, in1=gT[:, 16:32], op0=ALU.add, op1=ALU.mult),
            [("modT", 3), "gT"], [("AB", 4)])
        p.op("dve", lambda e: e.tensor_copy(out=AB[:, 5, :], in_=modT[:, 2, :, 0]), [("modT", 2)], [("AB", 5)])

    class Prep:
        def __init__(self, tiles, xn_bufs, xnname, junk, junkkey, xt_bufs=None, xtname=None):
            self.tiles = tiles
            self.xn = xn_bufs
            self.xnname = xnname
            self.junk = junk
            self.junkkey = junkkey
            self.xt = xt_bufs
            self.xtname = xtname

        def src(self, i):
            tl = self.tiles[i]
            if self.xt is not None:
                b = i % len(self.xt)
                return self.xt[b][:], [(self.xtname, b)]
            return tl["src"], tl["srckeys"]

        def load(self, i):
            if self.xt is None or i >= len(self.tiles):
                return
            b = i % len(self.xt)
            ti = self.tiles[i]["dram"]
            p.op("sp", lambda e: e.dma_start(out=self.xt[b][:], in_=xs[ti]), [], [(self.xtname, b)], slot=f"{self.xtname}{b}")

        def A(self, i):
            src, sk = self.src(i)
            c0 = 2 * (i % 4)
            p.op("act", lambda e: e.activation(out=self.junk[:], in_=src, func=AF.Square, accum_out=stat[:, c0:c0 + 1]),
                 sk, [self.junkkey, ("pst", c0)])

        def B(self, i):
            src, sk = self.src(i)
            c0 = 2 * (i % 4)
            b = i % len(self.xn)
            rs = stat[:, c0 + 1:c0 + 2]
            p.op("act", lambda e: e.activation(out=rs, in_=stat[:, c0:c0 + 1], func=AF.Ln, scale=1.0 / D, bias=EPS),
                 [("pst", c0)], [("pst", c0 + 1)])
            p.op("act", lambda e: e.activation(out=rs, in_=rs, func=AF.Exp, scale=-0.5), [("pst", c0 + 1)], [("pst", c0 + 1)])
            p.op("dve", lambda e: e.tensor_scalar(out=self.xn[b][:], in0=src, scalar1=rs, scalar2=None, op0=ALU.mult),
                 sk + [("pst", c0 + 1)], [(self.xnname, b)])

        def C(self, i):
            tl = self.tiles[i]
            b = i % len(self.xn)
            xn = self.xn[b]
            ai = tl["ai"]
            dst_of_k = tl["dst_of_k"]
            dstname = tl["dstkey"]
            extra_w = tl.get("extra_w", [])
            keyfn = tl.get("dstkeyfn", lambda k: (dstname, k))
            for g in range(2):
                pt, ptk = pb()
                for j in range(8):
                    k = g * 8 + j
                    p.op("pe", lambda e: e.transpose(out=pt[:, j * 128:(j + 1) * 128],
                                                     in_=xn[:, k * 128:(k + 1) * 128], identity=identb[:]),
                         [(self.xnname, b), "identb"], [ptk], mark=(j == 7))
                for j in range(8):
                    k = g * 8 + j
                    if g == 0:
                        p.op("act", lambda e: e.activation(
                            out=dst_of_k(k), in_=pt[:, j * 128:(j + 1) * 128], func=AF.Identity,
                            scale=AB[:, ai, k:k + 1], bias=AB[:, ai + 1, k:k + 1]),
                            [ptk, ("AB", ai), ("AB", ai + 1)], [keyfn(k)] + extra_w)
                    else:
                        p.op("dve", lambda e: e.tensor_scalar(
                            out=dst_of_k(k), in0=pt[:, j * 128:(j + 1) * 128],
                            scalar1=AB[:, ai, k:k + 1], scalar2=AB[:, ai + 1, k:k + 1], op0=ALU.mult, op1=ALU.add),
                            [ptk, ("AB", ai), ("AB", ai + 1)], [keyfn(k)] + extra_w)

        def steps(self):
            n = len(self.tiles)
            self.load(0)
            self.load(1)
            for i in range(n + 2):
                if 0 <= i - 2 < n:
                    self.C(i - 2)
                    yield
                if 0 <= i - 1 < n:
                    self.B(i - 1)
                    if i + 1 >= 2:
                        self.load(i + 1)
                    yield
                if i < n:
                    self.A(i)
                    yield

        def run_skewed(self):
            for _ in self.steps():
                pass

    def gate_sp(d, tok0, e_t, sp_t, spkey, ekey="e_t"):
        ps, pk = pf()
        p.op("pe", lambda e: e.matmul(ps[:], lhsT=laug[d][0:33, tok0:tok0 + 128], rhs=wga[d][0:33, :],
                                      start=True, stop=True), [("laug", d), ("wga", d)], [pk])
        p.op("act", lambda e: e.activation(out=e_t[:], in_=ps[:], func=AF.Exp, scale=-1.0), [pk], [ekey])
        p.op("act", lambda e: e.activation(out=sp_t[:], in_=e_t[:], func=AF.Ln, bias=1.0), [ekey], [spkey])

    def gate_state(d, sp_t, spkey, ktm_ap, ktmkey, ksc_t, kdec_t, dec_t, deckey, ksckey="ksc", kdeckey="kdec"):
        ps, pk = pf()
        p.op("pe", lambda e: e.matmul(ps[:], lhsT=STR[d], rhs=sp_t[:], start=True, stop=True),
             ["consts", spkey], [pk])
        p.op("act", lambda e: e.activation(out=ksc_t[:], in_=ps[:], func=AF.Exp, scale=-1.0 / 16), [pk], [ksckey])
        p.op("dve", lambda e: e.tensor_tensor(out=kdec_t[:], in0=ktm_ap, in1=ksc_t[:], op=ALU.mult),
             [ksckey, ktmkey], [kdeckey])
        ps2, pk2 = pf()
        for h in range(4):
            p.op("pe", lambda e, h=h: e.matmul(ps2[:, h:h + 1], lhsT=sp_t[:, h * 128:(h + 1) * 128], rhs=ones[:, 0:1],
                                                start=True, stop=True), [spkey, "ones"], [pk2], mark=(h == 3))
        p.op("act", lambda e: e.activation(out=dec_t, in_=ps2[:, 0:4], func=AF.Exp, scale=-1.0 / 16), [pk2], [deckey])

    def ds_matmuls(kdec_t, v_ap_of_h, vkey, kdeckey="kdec"):
        for half in range(2):
            ps, pk = psf[6 + half], ("psf", 6 + half)
            for hh in range(2):
                h = half * 2 + hh
                p.op("pe", lambda e, h=h, hh=hh: e.matmul(ps[:, hh * 256:(hh + 1) * 256],
                                                           lhsT=kdec_t[:, h * 128:(h + 1) * 128], rhs=v_ap_of_h(h),
                                                           start=True, stop=True), [kdeckey, vkey], [pk], mark=(hh == 1))

    def s_update(S, skey, dec_t, deckey, first):
        for half in range(2):
            ps, pk = psf[6 + half], ("psf", 6 + half)
            for hh in range(2):
                h = half * 2 + hh
                if first:
                    p.op("dve", lambda e, h=h, hh=hh: e.tensor_copy(out=S[:, h, :], in_=ps[:, hh * 256:(hh + 1) * 256]),
                         [pk], [(skey, h)])
                else:
                    p.op("dve", lambda e, h=h, hh=hh: e.scalar_tensor_tensor(
                        out=S[:, h, :], in0=S[:, h, :], scalar=dec_t[:, h:h + 1], in1=ps[:, hh * 256:(hh + 1) * 256],
                        op0=ALU.mult, op1=ALU.add), [pk, deckey, (skey, h)], [(skey, h)])

    def state_update(d, S, skey, kdec_t, v_ap_of_h, vkey, dec_t, deckey, first):
        ds_matmuls(kdec_t, v_ap_of_h, vkey)
        s_update(S, skey, dec_t, deckey, first)

    hxh = palloc("hxh", [128, KC, 1280], BF16, 0)
    xt = [palloc(f"xt{i}", [128, D], F32, 40960 + i * 8192) for i in range(2)]
    xnh = [palloc(f"xnh{i}", [128, D], BF16, 61440 + i * 4096) for i in range(NH)]
    ktm_h = palloc("ktm_h", [128, NH, 512], BF16, 65536)
    vtm_h = palloc("vtm_h", [128, NH, 1024], BF16, 75776)
    tA = 96256
    e_t = palloc("e_t", [128, 512], F32, tA)
    sp_t = palloc("sp_t", [128, 512], F32, tA + 2048)
    ksc_t = palloc("ksc_t", [128, 512], F32, tA + 4096)
    kdec_t = palloc("kdec_t", [128, 512], BF16, tA + 6144)
    Sh = [palloc(f"Sh{i}", [128, 4, 256], F32, 110592 + i * 4096) for i in range(4)]

    horder = list(range(8, 18))
    htiles = [dict(dram=ti, ai=(2 if ti >= 16 else 0),
                   dst_of_k=(lambda k, j=j: hxh[:, k, j * 128:(j + 1) * 128]), dstkey="hxh")
              for j, ti in enumerate(horder)]
    hprep = Prep(htiles, xnh, "xnh", junkbc, "junkbc", xt_bufs=xt, xtname="xt")
    hstate = {"i": 0}

    def halo_ab(cnt):
        for _ in range(cnt):
            i = hstate["i"]
            if i >= NH:
                return
            if i == 0:
                hprep.load(0)
                hprep.load(1)
            hprep.A(i)
            hprep.B(i)
            hprep.load(i + 2)
            hstate["i"] += 1

    def mod1_job(n):
        mj = mod_job(n)

        def fn(slab, skey):
            mj(slab, skey)
            halo_ab(2 if n < 2 else 1)
        return fn
    for n in range(8):
        add_job(wmod[:, n * 512:(n + 1) * 512], KC, 512, mod1_job(n))

    def halo_prep():
        halo_ab(NH)
        p.barrier()
        finish_mod1()
        for i in range(NH):
            hprep.C(i)
        p.barrier()

    def halo_l_job(slab, skey):
        halo_prep()
        for di, d in enumerate("fb"):
            for (t0, n) in ((0, 512), (512, 512), (1024, 256)):
                ps, pk = pf()
                for k in range(KC):
                    p.op("pe", lambda e, k=k: e.matmul(ps[0:16, 0:n], lhsT=slab[:, k, di * 16:(di + 1) * 16],
                                                        rhs=hxh[:, k, t0:t0 + n], start=(k == 0), stop=(k == KC - 1)),
                         [skey, ("hxh", k)], [pk], mark=(k == KC - 1))
                copy_op(evac_eng(), laug[d][0:16, t0:t0 + n], ps[0:16, 0:n], [pk], [("laug", d)])

    def halo_kv_job(which):
        def fn(slab, skey):
            for j in range(NH):
                ps, pk = pf()
                for k in range(KC):
                    p.op("pe", lambda e, k=k: e.matmul(ps[:], lhsT=hxh[:, k, j * 128:(j + 1) * 128], rhs=slab[:, k, :],
                                                        start=(k == 0), stop=(k == KC - 1)),
                         [("hxh", k), skey], [pk], mark=(k == KC - 1))
                if which == 0:
                    copy_op(evac_eng(), ktm_h[:, j, :], ps[:], [pk], [("ktm_h", j)])
                else:
                    copy_op(evac_eng(), vtm_h[:, j, (which - 1) * 512:which * 512], ps[:], [pk], [("vtm_h", j)])
            if which == 2:
                scan_and_prep()
        return fn

    e_t2 = palloc("e_t2", [128, 512], F32, 103424)
    sp_t2 = palloc("sp_t2", [128, 512], F32, 103424 + 2048)
    ksc_t2 = palloc("ksc_t2", [128, 512], F32, 103424 + 4096)
    kdec_t2 = palloc("kdec_t2", [128, 512], BF16, 103424 + 6144)

    def halo_scans():
        rr["n"] = 6
        segs = [(0, "f", [0, 1, 2, 3]), (1, "b", [7, 6, 5, 4]), (2, "f", [8, 9]), (3, "b", [9, 8])]
        tmps = [(e_t, sp_t, ksc_t, kdec_t, "A", stat[:, 8:12], ("stat", 8)),
                (e_t2, sp_t2, ksc_t2, kdec_t2, "B", stat[:, 28:32], ("stat", 28))]
        for pair in ((segs[0], segs[1]), (segs[2], segs[3])):
            for n in range(len(pair[0][2])):
                for ci, (si, d, tiles) in enumerate(pair):
                    j = tiles[n]
                    et_, spt_, ksct_, kdect_, nm, dect_, deck_ = tmps[ci]
                    gate_sp(d, j * 128, et_, spt_, "sp_t" + nm, ekey="e_t" + nm)
                yield
                for ci, (si, d, tiles) in enumerate(pair):
                    j = tiles[n]
                    et_, spt_, ksct_, kdect_, nm, dect_, deck_ = tmps[ci]
                    gate_state(d, spt_, "sp_t" + nm, ktm_h[:, j, :], ("ktm_h", j), ksct_, kdect_, dect_, deck_,
                               ksckey="ksc" + nm, kdeckey="kdec" + nm)
                yield
                for ci, (si, d, tiles) in enumerate(pair):
                    j = tiles[n]
                    et_, spt_, ksct_, kdect_, nm, dect_, deck_ = tmps[ci]
                    ds_matmuls(kdect_, lambda h, j=j: vtm_h[:, j, h * 256:(h + 1) * 256], ("vtm_h", j),
                               kdeckey="kdec" + nm)
                    s_update(Sh[si], ("Sh", si), dect_, deck_, first=(n == 0))
                yield
        for d, hi, ci, fi in (("f", 0, 2, 0), ("b", 1, 3, 1)):
            for h in range(4):
                p.op("dve", lambda e, h=h, hi=hi, ci=ci: e.tensor_tensor(
                    out=Sh[ci][:, h, :], in0=Sh[ci][:, h, :], in1=Sh[hi][:, h, :], op=ALU.subtract),
                    [(("Sh", ci), h), (("Sh", hi), h)], [(("Sh", ci), h)])
                p.op("dve", lambda e, h=h, hi=hi, ci=ci, d=d, fi=fi: e.scalar_tensor_tensor(
                    out=Sst[d][:, h, :], in0=Sh[ci][:, h, :], scalar=flags[:, fi:fi + 1], in1=Sh[hi][:, h, :],
                    op0=ALU.mult, op1=ALU.add),
                    [(("Sh", ci), h), (("Sh", hi), h), "flags"], [(("S", d), h)])

    add_job(win[:, 3072:3104], KC, 32, halo_l_job)
    add_job(win[:, 512:1024], KC, 512, halo_kv_job(0))
    add_job(win[:, 1024:1536], KC, 512, halo_kv_job(1))
    add_job(win[:, 1536:2048], KC, 512, halo_kv_job(2))

    mixconv = palloc("mixconv", [128, 8, T], BF16, 0)
    hxT = palloc("hxT", [128, KC, T], BF16, 16384)
    mixgla = palloc("mixgla", [128, 8, T], BF16, 16384)
    Sprevb = palloc("Sprevb", [128, NT, 4, 256], BF16, 32768)
    R1 = 49152
    xt2 = [palloc(f"xt2_{i}", [128, D], F32, R1 + i * 8192) for i in range(2)]
    xn2 = palloc("xn2", [128, D], BF16, R1 + 16384)
    u_t = palloc("u_t", [128, T], F32, R1)
    y_t = palloc("y_t", [128, T], F32, R1 + 4096)
    cb_sb = palloc("cb_sb", [128, 8, T], BF16, 69632)
    cc_sb = palloc("cc_sb", [128, 8, T], BF16, 69632 + 16384)
    v_sb = palloc("v_sb", [128, NT, 1024], BF16, 69632)
    sg_sb = palloc("sg_sb", [128, NT, 1024], BF16, 69632 + 16384)
    rawq = palloc("rawq", [128, 4, T], BF16, 102400)
    rawk = palloc("rawk", [128, 4, T], BF16, 102400 + 8192)
    rawktm = palloc("rawktm", [128, NT, 512], BF16, 102400 + 16384)
    g_tmp = palloc("g_tmp", [128, 256], F32, 126976)

    xn2a = palloc("xn2a", [128, D], BF16, 0)
    xn2b = palloc("xn2b", [128, D], BF16, 4096)

    def scan_and_prep():
        p.barrier()
        otiles = [dict(dram=t, ai=0, dst_of_k=(lambda k, t=t: hxT[:, k, t * 128:(t + 1) * 128]), dstkey="hxT")
                  for t in range(NT)]
        pit = Prep(otiles, [xn2a, xn2b], "xn2", junkbc, "junkbc", xt_bufs=xt2, xtname="xt2").steps()
        sit = halo_scans()
        alive = True
        while alive:
            alive = False
            try:
                next(sit)
                alive = True
            except StopIteration:
                pass
            for _ in range(2):
                try:
                    next(pit)
                    alive = True
                except StopIteration:
                    pass

    def own_prep():
        rr["n"] = 8
        p.barrier()

    def own_l_job(slab, skey):
        own_prep()
        for di, d in enumerate("fb"):
            for half in range(2):
                ps, pk = pf()
                for k in range(KC):
                    p.op("pe", lambda e, k=k: e.matmul(ps[0:16, :], lhsT=slab[:, k, di * 16:(di + 1) * 16],
                                                        rhs=hxT[:, k, half * 512:(half + 1) * 512],
                                                        start=(k == 0), stop=(k == KC - 1)),
                         [skey, ("hxT", k)], [pk], mark=(k == KC - 1))
                copy_op(evac_eng(), laug[d][0:16, half * 512:(half + 1) * 512], ps[0:16, :], [pk], [("laug", d)])

    def fm_job(evac):
        def fn(slab, skey):
            for c in range(4):
                for half in range(2):
                    ps, pk = pf()
                    for k in range(KC):
                        p.op("pe", lambda e, k=k: e.matmul(ps[:], lhsT=slab[:, k, c * 128:(c + 1) * 128],
                                                            rhs=hxT[:, k, half * 512:(half + 1) * 512],
                                                            start=(k == 0), stop=(k == KC - 1)),
                             [skey, ("hxT", k)], [pk], mark=(k == KC - 1))
                    evac(c, half, ps, pk)
        return fn

    def tm_job(evac):
        def fn(slab, skey):
            for t in range(NT):
                ps, pk = pf()
                for k in range(KC):
                    p.op("pe", lambda e, k=k: e.matmul(ps[:], lhsT=hxT[:, k, t * 128:(t + 1) * 128], rhs=slab[:, k, :],
                                                        start=(k == 0), stop=(k == KC - 1)),
                         [("hxT", k), skey], [pk], mark=(k == KC - 1))
                evac(t, ps, pk)
        return fn

    def ev_cb(s):
        def f(c, half, ps, pk):
            copy_op(evac_eng(), cb_sb[:, s * 4 + c, half * 512:(half + 1) * 512], ps[:], [pk], [("cb", s * 4 + c)])
        return f

    def ev_cc(s):
        def f(c, half, ps, pk):
            copy_op(evac_eng(), cc_sb[:, s * 4 + c, half * 512:(half + 1) * 512], ps[:], [pk], [("cc", s * 4 + c)])
        return f

    def ev_cx(s):
        def f(c, half, ps, pk):
            ch = s * 4 + c
            p.op("dve", lambda e: e.tensor_tensor(out=u_t[:, half * 512:(half + 1) * 512], in0=ps[:],
                                                  in1=cc_sb[:, ch, half * 512:(half + 1) * 512], op=ALU.mult),
                 [pk, ("cc", ch)], [("u", half)])
            if half == 1:
                u3 = u_t[:].rearrange("p (r w) -> p r w", w=64)
                y3 = y_t[:].rearrange("p (r w) -> p r w", w=64)
                p.op("act", lambda e: e.activation(out=y_t[:], in_=u_t[:], func=AF.Copy, scale=cwT[:, 8 + ch:9 + ch]),
                     [("u", 0), ("u", 1), "cwT"], ["y"])
                p.op("dve", lambda e: e.scalar_tensor_tensor(out=y3[:, :, 1:64], in0=u3[:, :, 0:63],
                                                             scalar=cwT[:, ch:ch + 1], in1=y3[:, :, 1:64],
                                                             op0=ALU.mult, op1=ALU.add),
                     [("u", 0), ("u", 1), "cwT", "y"], ["y"])
                p.op("dve", lambda e: e.scalar_tensor_tensor(out=y3[:, :, 0:63], in0=u3[:, :, 1:64],
                                                             scalar=cwT[:, 16 + ch:17 + ch], in1=y3[:, :, 0:63],
                                                             op0=ALU.mult, op1=ALU.add),
                     [("u", 0), ("u", 1), "cwT", "y"], ["y"])
                p.op("dve", lambda e: e.tensor_tensor(out=mixconv[:, ch, :], in0=y_t[:], in1=cb_sb[:, ch, :], op=ALU.mult),
                     ["y", ("cb", ch)], [("mixconv", ch)])
        return f

    def ev_q(c, half, ps, pk):
        copy_op(evac_eng(), rawq[:, c, half * 512:(half + 1) * 512], ps[:], [pk], [("rawq", c, half)], scale=128.0 ** -0.5)

    def ev_k(c, half, ps, pk):
        copy_op(evac_eng(), rawk[:, c, half * 512:(half + 1) * 512], ps[:], [pk], [("rawk", c, half)])

    def ev_ktm(t, ps, pk):
        copy_op(evac_eng(), rawktm[:, t, :], ps[:], [pk], [("rawktm", t)])

    def ev_v(s):
        def f(t, ps, pk):
            copy_op(evac_eng(), v_sb[:, t, s * 512:(s + 1) * 512], ps[:], [pk], [("v", t)])
        return f

    def ev_g(s):
        def f(t, ps, pk):
            dst = sg_sb[:, t, s * 512:(s + 1) * 512]
            p.op("act", lambda e: e.activation(out=dst, in_=ps[:], func=AF.Silu), [pk], [("sg", t, s)])
            p.op("pool", lambda e: e.tensor_tensor(
                out=dst.rearrange("p (h v) -> p h v", h=2), in0=dst.rearrange("p (h v) -> p h v", h=2),
                in1=glagb[:].unsqueeze(1).to_broadcast([128, 2, 256]), op=ALU.mult),
                [("sg", t, s), "glagb"], [("sg", t, s)])
        return f

    add_job(win[:, 3072:3104], KC, 32, own_l_job)
    mod_mid = list(range(8, 19))

    def add_mod_mid():
        if mod_mid:
            n = mod_mid.pop(0)
            add_job(wmod[:, n * 512:(n + 1) * 512], KC, 512, mod_job(n))
    for s in range(2):
        add_job(win[:, 3104 + s * 512:3104 + (s + 1) * 512], KC, 512, fm_job(ev_cb(s)))
        add_mod_mid()
    for s in range(2):
        add_job(win[:, 4128 + s * 512:4128 + (s + 1) * 512], KC, 512, fm_job(ev_cc(s)))
        add_mod_mid()
    for s in range(2):
        add_job(win[:, 5152 + s * 512:5152 + (s + 1) * 512], KC, 512, fm_job(ev_cx(s)))
        add_mod_mid()
    add_job(win[:, 0:512], KC, 512, fm_job(ev_q))
    add_mod_mid()
    def k_job(slab, skey):
        fm_job(ev_k)(slab, skey)
        for t in range(NT):
            pt, ptk = pb()
            for h in range(4):
                p.op("pe", lambda e: e.transpose(out=pt[:, h * 128:(h + 1) * 128], in_=rawk[:, h, t * 128:(t + 1) * 128],
                                                 identity=identb[:]),
                     [("rawk", h, t // 4), "identb"], [ptk], mark=(h == 3))
            copy_op(evac_eng(), rawktm[:, t, :], pt[:, 0:512], [ptk], [("rawktm", t)])
    add_job(win[:, 512:1024], KC, 512, k_job)
    add_mod_mid()

    def v_job0(slab, skey):
        p.barrier()
        tm_job(ev_v(0))(slab, skey)
    add_job(win[:, 1024:1536], KC, 512, v_job0)
    add_mod_mid()
    add_job(win[:, 1536:2048], KC, 512, tm_job(ev_v(1)))
    add_mod_mid()
    add_job(win[:, 2048:2560], KC, 512, tm_job(ev_g(0)))
    add_mod_mid()

    MASK = TRI

    GJ = {"slab_off": None}

    def gla_alloc():
        so = GJ["slab_off"]
        G = {}
        for ci in range(2):
            G["e", ci] = palloc(f"g_e{ci}", [128, 512], F32, R1 + ci * 2048)
            G["sp", ci] = palloc(f"g_sp{ci}", [128, 512], F32, R1 + 4096 + ci * 2048)
            G["E1", ci] = palloc(f"g_E1{ci}", [128, 512], F32, R1 + 8192 + ci * 2048)
            G["ksc", ci] = palloc(f"g_ksc{ci}", [128, 512], F32, R1 + 12288 + ci * 2048)
            G["ke", ci] = alloc(f"g_ke{ci}", [128, 4, 128], BF16, off=so + 4096 + ci * 1024)
            G["kdec", ci] = alloc(f"g_kdec{ci}", [128, 512], BF16, off=so + 6144 + ci * 1024)
            for par in range(2):
                G["qe", ci, par] = palloc(f"g_qe{ci}{par}", [128, 4, 128], BF16, R1 + 16384 + (2 * ci + par) * 1024)
                G["sT", ci, par] = alloc(f"g_sT{ci}{par}", [128, 4, 128], BF16, off=so + (2 * ci + par) * 1024)
        G["og"] = alloc("g_og", [128, 1024], BF16, off=so + 8192)
        return G

    def gla():
        flush_mod_tail()
        rr["n"] = 4
        p.barrier()
        G = gla_alloc()
        DIR = ("f", "b")

        tl = list(range(NT - 1, -1, -1))
        for pi in range(0, NT, 2):
            if pi > 0:
                yield
            pair = tl[pi:pi + 2]
            for ci, t in enumerate(pair):
                gate_sp("b", t * 128, G["e", ci], G["sp", ci], ("g_sp", ci), ekey=("g_e", ci))
            for ci, t in enumerate(pair):
                gate_state("b", G["sp", ci], ("g_sp", ci), rawktm[:, t, :], ("rawktm", t), G["ksc", ci], G["kdec", ci],
                           stat[:, 8 + 4 * ci:12 + 4 * ci], ("stat", 8 + 4 * ci),
                           ksckey=("g_ksc", ci), kdeckey=("g_kdec", ci))
            for ci, t in enumerate(pair):
                for h in range(4):
                    eng = "act" if h % 2 == 0 else "dve"
                    copy_op(eng, Sprevb[:, t, h, :], Sst["b"][:, h, :], [(("S", "b"), h)], [("Sprevb", t)])
                ds_matmuls(G["kdec", ci], lambda h, t=t: v_sb[:, t, h * 256:(h + 1) * 256], ("v", t),
                           kdeckey=("g_kdec", ci))
                s_update(Sst["b"], ("S", "b"), stat[:, 8 + 4 * ci:12 + 4 * ci], ("stat", 8 + 4 * ci), first=False)

        def S1(t):
            for ci, d in enumerate(DIR):
                gate_sp(d, t * 128, G["e", ci], G["sp", ci], ("g_sp", ci), ekey=("g_e", ci))

        def S2(t, par):
            banks = []
            for ci, d in enumerate(DIR):
                ps, pk = pf()
                banks.append((ps, pk))
                for h in range(4):
                    p.op("pe", lambda e: e.matmul(ps[:, h * 128:(h + 1) * 128],
                                                  lhsT=G["sp", ci][:, h * 128:(h + 1) * 128], rhs=TRI[d],
                                                  start=True, stop=True),
                         [("g_sp", ci), "consts"], [pk], mark=(h == 3))
            for ci, d in enumerate(DIR):
                ps, pk = banks[ci]
                p.op("act", lambda e: e.activation(out=G["E1", ci][:], in_=ps[:], func=AF.Exp, scale=-1.0 / 16),
                     [pk], [("g_E1", ci)])
                p.op("act", lambda e: e.activation(out=G["e", ci][:], in_=ps[:], func=AF.Exp, scale=1.0 / 16),
                     [pk], [("g_e", ci)])
            for ci, d in enumerate(DIR):
                p.op("dve", lambda e: e.tensor_tensor(
                    out=G["qe", ci, par][:], in0=rawq[:, :, t * 128:(t + 1) * 128],
                    in1=G["E1", ci][:].rearrange("p (h i) -> p h i", h=4), op=ALU.mult),
                    [("g_E1", ci)] + [("rawq", c, t // 4) for c in range(4)], [("qe", ci, par)])
                p.op("dve", lambda e: e.tensor_tensor(
                    out=G["ke", ci][:], in0=rawk[:, :, t * 128:(t + 1) * 128],
                    in1=G["e", ci][:].rearrange("p (h i) -> p h i", h=4), op=ALU.mult),
                    [("g_e", ci)] + [("rawk", c, t // 4) for c in range(4)], [("ke", ci)])

        def S3(t, par, dec_t, deckey):
            psa, pka = pf()
            p.op("pe", lambda e: e.matmul(psa[:], lhsT=STR["f"], rhs=G["sp", 0][:], start=True, stop=True),
                 ["consts", ("g_sp", 0)], [pka])
            pst, pkt = pf()
            for h in range(4):
                p.op("pe", lambda e: e.matmul(pst[:, h:h + 1], lhsT=G["sp", 0][:, h * 128:(h + 1) * 128], rhs=ones[:, 0:1],
                                              start=True, stop=True), [("g_sp", 0), "ones"], [pkt], mark=(h == 3))
            sc = []
            for ci, d in enumerate(DIR):
                ps2, pk2 = pf()
                sc.append((ps2, pk2))
                for h in range(4):
                    p.op("pe", lambda e: e.matmul(ps2[:, h * 128:(h + 1) * 128], lhsT=G["ke", ci][:, h, :],
                                                  rhs=G["qe", ci, par][:, h, :], start=True, stop=True),
                         [("ke", ci), ("qe", ci, par)], [pk2], mark=(h == 3))
            p.op("act", lambda e: e.activation(out=G["ksc", 0][:], in_=psa[:], func=AF.Exp, scale=-1.0 / 16),
                 [pka], [("g_ksc", 0)])
            p.op("act", lambda e: e.activation(out=dec_t, in_=pst[:, 0:4], func=AF.Exp, scale=-1.0 / 16), [pkt], [deckey])
            p.op("dve", lambda e: e.tensor_tensor(out=G["kdec", 0][:], in0=rawktm[:, t, :], in1=G["ksc", 0][:], op=ALU.mult),
                 [("g_ksc", 0), ("rawktm", t)], [("g_kdec", 0)])
            for ci, d in enumerate(DIR):
                ps2, pk2 = sc[ci]
                p.op("dve", lambda e: e.tensor_tensor(
                    out=G["sT", ci, par][:], in0=ps2[:].rearrange("p (h i) -> p h i", h=4),
                    in1=MASK[d].unsqueeze(1).to_broadcast([128, 4, 128]), op=ALU.mult),
                    [pk2, "consts"], [("sT", ci, par)])

        def sfb_copy():
            for h in range(4):
                eng = "act" if h % 2 == 0 else "dve"
                copy_op(eng, Sfb[:, h, :], Sst["f"][:, h, :], [(("S", "f"), h)], [("Sfb", h)])

        def S4(t, par, dec_t, deckey):
            ds_matmuls(G["kdec", 0], lambda h: v_sb[:, t, h * 256:(h + 1) * 256], ("v", t), kdeckey=("g_kdec", 0))
            for half in range(2):
                ps, pk = psf[4 + half], ("psf", 4 + half)
                for hh in range(2):
                    h = half * 2 + hh
                    o_ap = ps[:, hh * 256:(hh + 1) * 256]
                    v_ap = v_sb[:, t, h * 256:(h + 1) * 256]
                    p.op("pe", lambda e: e.matmul(o_ap, lhsT=G["sT", 0, par][:, h, :], rhs=v_ap, start=True, stop=False),
                         [("sT", 0, par), ("v", t)], [pk], mark=False)
                    p.op("pe", lambda e: e.matmul(o_ap, lhsT=G["qe", 0, par][:, h, :], rhs=Sfb[:, h, :],
                                                  start=False, stop=False),
                         [("qe", 0, par), ("Sfb", h)], [pk], mark=False)
                    p.op("pe", lambda e: e.matmul(o_ap, lhsT=G["sT", 1, par][:, h, :], rhs=v_ap, start=False, stop=False),
                         [("sT", 1, par), ("v", t)], [pk], mark=False)
                    p.op("pe", lambda e: e.matmul(o_ap, lhsT=G["qe", 1, par][:, h, :], rhs=Sprevb[:, t, h, :],
                                                  start=False, stop=True),
                         [("qe", 1, par), ("Sprevb", t)], [pk], mark=(hh == 1))
            s_update(Sst["f"], ("S", "f"), dec_t, deckey, first=False)
            if t + 1 < NT:
                sfb_copy()

        def S5a(t):
            g_og = G["og"]
            for h in range(4):
                ps, pk = psf[4 + h // 2], ("psf", 4 + h // 2)
                o_ap = ps[:, (h % 2) * 256:(h % 2 + 1) * 256]
                p.op("act", lambda e: e.activation(out=g_og[:, h * 256:(h + 1) * 256], in_=o_ap,
                                                   func=AF.Square, accum_out=stat[:, 16 + h:17 + h]),
                     [pk], [("og", h), ("stat", 16 + h)])
            p.op("act", lambda e: e.activation(out=stat[:, 20:24], in_=stat[:, 16:20], func=AF.Ln, scale=1.0 / 256, bias=EPS),
                 [("stat", 16 + h) for h in range(4)], [("stat", 20)])
            p.op("act", lambda e: e.activation(out=stat[:, 20:24], in_=stat[:, 20:24], func=AF.Exp, scale=-0.5),
                 [("stat", 20)], [("stat", 20)])
            for h in range(4):
                ps, pk = psf[4 + h // 2], ("psf", 4 + h // 2)
                o_ap = ps[:, (h % 2) * 256:(h % 2 + 1) * 256]
                p.op("dve", lambda e: e.scalar_tensor_tensor(
                    out=g_og[:, h * 256:(h + 1) * 256], in0=o_ap, scalar=stat[:, 20 + h:21 + h],
                    in1=sg_sb[:, t, h * 256:(h + 1) * 256], op0=ALU.mult, op1=ALU.mult),
                    [pk, ("stat", 20), ("sg", t, h // 2)], [("og", h)])

        def S5b(t):
            g_og = G["og"]
            pt, ptk = pb()
            for j in range(8):
                p.op("pe", lambda e: e.transpose(out=pt[:, j * 128:(j + 1) * 128], in_=g_og[:, j * 128:(j + 1) * 128],
                                                 identity=identb[:]), [("og", j // 2), "identb"], [ptk], mark=(j == 7))
            copy_op(evac_eng(), mixgla[:, :, t * 128:(t + 1) * 128], pt[:].rearrange("p (j i) -> p j i", j=8),
                    [ptk], [("mixgla", t)])

        sfb_copy()
        for t in range(NT):
            yield
            par = t % 2
            dec_t = stat[:, 40:44] if par == 0 else stat[:, 44:48]
            deckey = ("stat", 40 if par == 0 else 44)
            S1(t)
            S2(t, par)
            S3(t, par, dec_t, deckey)
            if t > 0:
                S5b(t - 1)
            S4(t, par, dec_t, deckey)
            S5a(t)
        S5b(NT - 1)
        p.barrier()

    gla_it = gla()

    def gla_step(k=1):
        for _ in range(k):
            try:
                next(gla_it)
            except StopIteration:
                pass

    def g_job1(slab, skey):
        tm_job(ev_g(1))(slab, skey)
        gla_step(64)
    GJ["slab_off"] = SB_BASE + (len(jobs) % NSLAB) * (KC * 512 * 2)
    add_job(win[:, 2560:3072], KC, 512, g_job1)

    xres = palloc("xres", [128, NT, D], F32, 32768)
    etmp = [palloc(f"etmp{i}", [128, 512], F32, 98304 + i * 2048) for i in range(2)]
    et = {"i": 0}

    def resid_evac(ps, pk, t, n):
        i = et["i"]
        et["i"] ^= 1
        p.op("dve", lambda e: e.tensor_tensor(out=etmp[i][:], in0=ps[:], in1=bc[:, n * 512:(n + 1) * 512], op=ALU.mult),
             [pk, ("bc", n)], [("etmp", i)])
        p.op("pool", lambda e: e.tensor_tensor(out=xres[:, t, n * 512:(n + 1) * 512], in0=xres[:, t, n * 512:(n + 1) * 512],
                                               in1=etmp[i][:], op=ALU.add),
             [("etmp", i), ("xres", t)], [("xres", t)])

    def outproj_job(n):
        def fn(slab, skey):
            if n == 0:
                rr["n"] = 8
                p.barrier()
                for t in range(NT):
                    p.op("sp", lambda e, t=t: e.dma_start(out=xres[:, t, :], in_=xs[t]), [], [("xres", t)], slot="xres")
                p.retoken([("xres", t) for t in range(NT)], "xres")
            fprep = None
            if n == 3:
                finish_mod2()
                ftiles = [dict(src=xres[:, t, :], srckeys=[("xres", t)], ai=4,
                               dst_of_k=(lambda k, t=t: h2T[:, k, t * 128:(t + 1) * 128]), dstkey="h2T",
                               dstkeyfn=(lambda k, t=t: ("h2T", k, t // 4)),
                               extra_w=[("mixcol", t)]) for t in range(NT)]
                fprep = Prep(ftiles, [xn3a, xn3b], "xn3", junk3, "junk3")
            for t in range(NT):
                ps, pk = pf()
                for k in range(KC):
                    src = mixgla if k < 8 else mixconv
                    p.op("pe", lambda e, k=k, src=src: e.matmul(ps[:], lhsT=src[:, k % 8, t * 128:(t + 1) * 128],
                                                                 rhs=slab[:, k, :], start=(k == 0), stop=(k == KC - 1)),
                         [("mixcol", t), skey], [pk], mark=(k == KC - 1))
                resid_evac(ps, pk, t, n)
                if fprep is not None:
                    if t - 1 >= 0:
                        fprep.A(t - 1)
                    if t - 2 >= 0:
                        fprep.B(t - 2)
                    if t - 3 >= 0:
                        fprep.C(t - 3)
            if fprep is not None:
                fprep.A(NT - 1)
                fprep.B(NT - 2)
                fprep.C(NT - 3)
                fprep.B(NT - 1)
                fprep.C(NT - 2)
                fprep.C(NT - 1)
        return fn
    for n in range(4):
        add_job(wout[:, n * 512:(n + 1) * 512], KC, 512, outproj_job(n))
        if n == 0:
            add_job(wmod[:, 19 * 512:20 * 512], KC, 512, mod_job(19))


    h2T = palloc("h2T", [128, KC, T], BF16, 0)
    hT = palloc("hT", [128, 8, T], BF16, 98304)
    xn3 = palloc("xn3", [128, D], BF16, 98304)
    sgate = palloc("sgate", [128, 4, T], BF16, 114688)
    etmp2 = [palloc(f"etmp2_{i}", [128, 512], F32, 122880 + i * 2048) for i in range(2)]

    xn3a = palloc("xn3a", [128, D], BF16, 102400)
    xn3b = palloc("xn3b", [128, D], BF16, 106496)
    junk3 = palloc("junk3", [128, D], BF16, 110592)

    def ffn_prep():
        p.barrier()

    pieces = [(0, 8), (8, 8), (16, 8), (24, 8), (32, 8), (40, 4)]

    def gate_job(pi, s, first):
        def fn(slab, skey):
            for c in range(4):
                for half in range(2):
                    ps, pk = pf()
                    for k in range(KC):
                        p.op("pe", lambda e, k=k: e.matmul(ps[:], lhsT=slab[:, k, c * 128:(c + 1) * 128],
                                                            rhs=h2T[:, k, half * 512:(half + 1) * 512],
                                                            start=(k == 0), stop=(k == KC - 1)),
                             [skey, ("h2T", k, half)], [pk], mark=(k == KC - 1))
                    p.op("act", lambda e: e.activation(out=sgate[:, c, half * 512:(half + 1) * 512], in_=ps[:], func=AF.Silu),
                         [pk], [("sgate", c, half)])
        return fn

    def up_job(pi, s):
        def fn(slab, skey):
            if pi == 0 and s == 0:
                ffn_prep()
            for c in range(4):
                for half in range(2):
                    ps, pk = pf()
                    for k in range(KC):
                        p.op("pe", lambda e, k=k: e.matmul(ps[:], lhsT=slab[:, k, c * 128:(c + 1) * 128],
                                                            rhs=h2T[:, k, half * 512:(half + 1) * 512],
                                                            start=(k == 0), stop=(k == KC - 1)),
                             [skey, ("h2T", k, half)], [pk], mark=(k == KC - 1))
                    p.op("dve", lambda e: e.tensor_tensor(out=hT[:, s * 4 + c, half * 512:(half + 1) * 512], in0=ps[:],
                                                          in1=sgate[:, c, half * 512:(half + 1) * 512], op=ALU.mult),
                         [pk, ("sgate", c, half)], [("hT", s * 4 + c)])
        return fn

    et2 = {"i": 0}

    def down_job(pi, n, nch):
        def fn(slab, skey):
            flush_mod_tail()
            for t in range(NT):
                ps, pk = pf()
                for c in range(nch):
                    p.op("pe", lambda e, c=c: e.matmul(ps[:], lhsT=hT[:, c, t * 128:(t + 1) * 128], rhs=slab[:, c, :],
                                                        start=(c == 0), stop=(c == nch - 1)),
                         [("hT", c), skey], [pk], mark=(c == nch - 1))
                i = et2["i"]
                et2["i"] ^= 1
                p.op("dve", lambda e, i=i: e.tensor_tensor(out=etmp2[i][:], in0=ps[:], in1=bc[:, n * 512:(n + 1) * 512],
                                                          op=ALU.mult), [pk, ("bc", n)], [("etmp2", i)])
                p.op("pool", lambda e, i=i: e.tensor_tensor(out=xres[:, t, n * 512:(n + 1) * 512],
                                                           in0=xres[:, t, n * 512:(n + 1) * 512], in1=etmp2[i][:], op=ALU.add),
                     [("etmp2", i), ("xres", t)], [("xres", t)])
        return fn

    gt2_jobs = list(range(20, 24))
    for pi, (c0, nch) in enumerate(pieces):
        for s in range(nch // 4):
            col = (c0 + s * 4) * 128
            add_job(wfg[:, col:col + 512], KC, 512, gate_job(pi, s, first=(pi == 0 and s == 0)))
            if gt2_jobs:
                n = gt2_jobs.pop(0)
                add_job(wmod[:, n * 512:(n + 1) * 512], KC, 512, mod_job(n))
            add_job(wfu[:, col:col + 512], KC, 512, up_job(pi, s))
            if gt2_jobs:
                n = gt2_jobs.pop(0)
                add_job(wmod[:, n * 512:(n + 1) * 512], KC, 512, mod_job(n))
        for n in range(4):
            add_job(wfd[c0 * 128:(c0 + nch) * 128, n * 512:(n + 1) * 512], nch, 512, down_job(pi, n, nch))

    run_jobs()

    p.barrier()
    p.op("sp", lambda e: e.dma_start(out=bc[:], in_=fg_d.partition_broadcast(128)), [], ["fgb"], slot="fgb")
    ot = [palloc(f"ot{i}", [128, D], F32, i * 8192) for i in range(2)]
    junk = palloc("junk", [128, D], BF16, 16384)
    for t in range(NT):
        i = t % 2
        ss = stat[:, 32 + 2 * i:33 + 2 * i]
        rs = stat[:, 33 + 2 * i:34 + 2 * i]
        p.op("act", lambda e, ss=ss: e.activation(out=junk[:], in_=xres[:, t, :], func=AF.Square, accum_out=ss),
             [("xres", t)], ["junk", ("fs", i)])
        p.op("act", lambda e, ss=ss, rs=rs: e.activation(out=rs, in_=ss, func=AF.Ln, scale=1.0 / D, bias=EPS),
             [("fs", i)], [("fr", i)])
        p.op("act", lambda e, rs=rs: e.activation(out=rs, in_=rs, func=AF.Exp, scale=-0.5), [("fr", i)], [("fr", i)])
        p.op("dve", lambda e, rs=rs, i=i: e.scalar_tensor_tensor(out=ot[i][:], in0=xres[:, t, :], scalar=rs, in1=bc[:],
                                                                op0=ALU.mult, op1=ALU.mult),
             [("xres", t), ("fr", i), "fgb"], [("ot", i)])
        p.op("sp", lambda e, i=i: e.dma_start(out=out_d[t], in_=ot[i][:]), [("ot", i)], [], slot=f"out{i}")
    p.finish(["out0", "out1"])

    with contextlib.ExitStack() as stack:
        p.emit(nc, stack)
    return nc


def _host_inputs(inputs):
    f = lambda a: np.ascontiguousarray(np.asarray(a, dtype=np.float32))
    x = f(inputs["x"])[0]
    ctx = f(inputs["ctx"])[0]
    idx = np.arange(128)
    consts = np.zeros((128, 5, 128), np.float32)
    consts[:, 0, :] = np.eye(128, dtype=np.float32)
    consts[:, 1, :] = (idx[:, None] <= idx[None, :])
    consts[:, 2, :] = (idx[:, None] >= idx[None, :])
    consts[:, 3, :] = (idx[:, None] > idx[None, :])
    consts[:, 4, :] = (idx[:, None] < idx[None, :])
    sel = np.zeros((2, 128), np.float32)
    sel[0] = 1.0
    common = {
        "consts": consts, "sel": sel,
        "c2": np.concatenate([f(inputs["c"]).reshape(16, 128), f(inputs["c_ctx"]).reshape(16, 128)], 0),
        "ng": np.concatenate([f(inputs["norm1_g"]).reshape(16, 128), f(inputs["norm2_g"]).reshape(16, 128)], 0),
        "cw": f(inputs["conv_w"]).reshape(24, 128),
        "w_mod": f(inputs["w_mod"])[0], "b_mod": f(inputs["b_mod"]).reshape(1, -1),
        "w_in": f(inputs["w_in"])[0],
        "w_gate_f": f(inputs["w_gate_f"])[0], "w_gate_b": f(inputs["w_gate_b"])[0],
        "b_gate_f": f(inputs["b_gate_f"]).reshape(1, 512), "b_gate_b": f(inputs["b_gate_b"]).reshape(1, 512),
        "gla_norm_g": f(inputs["gla_norm_g"]).reshape(256),
        "w_out": f(inputs["w_out"])[0],
        "w_ffn_gate": f(inputs["w_ffn_gate"])[0], "w_ffn_up": f(inputs["w_ffn_up"])[0],
        "w_ffn_down": f(inputs["w_ffn_down"])[0],
        "final_g": f(inputs["final_g"]).reshape(D),
    }
    maps = []
    zeros = np.zeros((HALO, D), np.float32)
    for c in range(8):
        s = c * T
        own = x[s:s + T]
        before = x[s - HALO:s] if c > 0 else zeros
        after = x[s + T:s + T + HALO] if c < 7 else zeros
        xs = np.concatenate([own, before, after, ctx], 0).reshape(18, 128, D)
        flags = np.zeros((128, 2), np.float32)
        flags[:, 0] = 1.0 if c == 0 else 0.0
        flags[:, 1] = 1.0 if c == 7 else 0.0
        m = dict(common)
        m["xs"] = np.ascontiguousarray(xs)
        m["flags"] = flags
        maps.append(m)
    return maps


_NC_CACHE = {}


def kernel(**inputs):
    maps = _host_inputs(inputs)
    if "nc" not in _NC_CACHE:
        _NC_CACHE["nc"] = build()
    nc = _NC_CACHE["nc"]
    res = run_bass_kernel_spmd(nc, maps, core_ids=list(range(8)))
    outs = [np.asarray(r["out"]).reshape(T, D) for r in res.results]
    return np.concatenate(outs, 0).reshape(1, 8 * T, D).astype(np.float32)
```
